# Optimizing a Trainium2 kernel written in Bass

```python
import math
import jax, jax.numpy as jnp
from jax import lax
import numpy as np

D_MODEL = 2048
BATCH = 4
SEQ = 2048
DEPTH = 4

CHUNK = 64
Q_BLOCK = 128
HEAD_DIM = 128
H_MLA = 8
MLA_NOPE = 128
MLA_ROPE = 64
MLA_V = 128
Q_LORA = 512
KV_LORA = 256
ROPE_THETA = 10000.0
H_FOX = 8
H_SB = 8
BRANCH_W = H_MLA * MLA_V
N_BRANCH = 3
DEEPNORM_ALPHA = (2 * DEPTH) ** 0.25
DEEPNORM_BETA = (8 * DEPTH) ** -0.25
LN_EPS = 1e-5
RMS_EPS = 1e-6
NEG_INF = -1e30
SPLIT_SIZES = (Q_LORA, KV_LORA, MLA_ROPE, BRANCH_W,
               3 * BRANCH_W, H_FOX, BRANCH_W,
               3 * BRANCH_W, BRANCH_W,
               N_BRANCH * D_MODEL)
D_IN = sum(SPLIT_SIZES)

kernel_name = 'hybrid_mla_fox_stickbreaking_deepnorm_adaln'


def layer_norm(x, g=None, b=None):
    xf = x.astype(jnp.float32)
    mu = jnp.mean(xf, axis=-1, keepdims=True)
    var = jnp.mean(jnp.square(xf - mu), axis=-1, keepdims=True)
    y = (xf - mu) * lax.rsqrt(var + LN_EPS)
    if g is not None:
        y = y * g.astype(jnp.float32) + b.astype(jnp.float32)
    return y.astype(x.dtype)


def rms_norm(x, g):
    xf = x.astype(jnp.float32)
    y = xf * lax.rsqrt(jnp.mean(xf * xf, axis=-1, keepdims=True) + RMS_EPS)
    return (y * g.astype(jnp.float32)).astype(x.dtype)


def rope(x, positions):
    half = x.shape[-1] // 2
    inv_freq = ROPE_THETA ** (-jnp.arange(half, dtype=jnp.float32) / half)
    ang = positions.astype(jnp.float32)[..., None] * inv_freq
    cos = jnp.cos(ang)[:, :, None, :]
    sin = jnp.sin(ang)[:, :, None, :]
    xf = x.astype(jnp.float32)
    x1, x2 = xf[..., :half], xf[..., half:]
    return jnp.concatenate([x1 * cos - x2 * sin, x2 * cos + x1 * sin], axis=-1).astype(x.dtype)


def split_cols(p, sizes):
    idx, acc = [], 0
    for s in sizes[:-1]:
        acc += s
        idx.append(acc)
    return jnp.split(p, idx, axis=-1)


def sweep_query_blocks(block_fn, q, k, v):
    outs = []
    for i in range(q.shape[1] // Q_BLOCK):
        q0 = i * Q_BLOCK
        q1 = q0 + Q_BLOCK
        outs.append(block_fn(q[:, q0:q1], k[:, :q1], v[:, :q1], q0))
    return jnp.concatenate(outs, axis=1)


def mla_block(q, k, v, q0):
    s = jnp.einsum('bqhd,bkhd->bhqk', q, k).astype(jnp.float32) / math.sqrt(MLA_NOPE + MLA_ROPE)
    t = q0 + jnp.arange(q.shape[1])
    src = jnp.arange(k.shape[1])
    mask = (src[None, :] // CHUNK) <= (t[:, None] // CHUNK)
    p = jax.nn.softmax(jnp.where(mask, s, NEG_INF), axis=-1).astype(v.dtype)
    return jnp.einsum('bhqk,bkhd->bqhd', p, v)


def fox_block(q, k, v, q0, fcum):
    n_q, n_k = q.shape[1], k.shape[1]
    s = jnp.einsum('bqhd,bkhd->bhqk', q, k).astype(jnp.float32) / math.sqrt(HEAD_DIM)
    s = s + fcum[:, :, q0:q0 + n_q, None] - fcum[:, :, None, :n_k]
    t = q0 + jnp.arange(n_q)
    src = jnp.arange(n_k)
    mask = src[None, :] <= t[:, None]
    p = jax.nn.softmax(jnp.where(mask, s, NEG_INF), axis=-1).astype(v.dtype)
    return jnp.einsum('bhqk,bkhd->bqhd', p, v)


def stick_breaking_block(q, k, v, q0):
    z = jnp.einsum('bqhd,bkhd->bhqk', q, k).astype(jnp.float32) / math.sqrt(HEAD_DIM)
    t = q0 + jnp.arange(q.shape[1])
    src = jnp.arange(k.shape[1])
    strict = src[None, :] < t[:, None]
    log_beta = jax.nn.log_sigmoid(z)
    log_1mb = jnp.where(strict, jax.nn.log_sigmoid(-z), 0.0)
    after = lax.cumsum(log_1mb, axis=log_1mb.ndim - 1, reverse=True) - log_1mb
    a = jnp.where(strict, jnp.exp(log_beta + after), 0.0).astype(v.dtype)
    return jnp.einsum('bhqk,bkhd->bqhd', a, v)


def hybrid_layer(x, c, positions, w_ada, b_ada, w_in, q_norm_g, kv_norm_g, w_uq, w_ukv,
                 fox_bias, w_branch, w_out, ln_g, ln_b):
    bsz, seq, _ = x.shape
    mod = c @ w_ada + b_ada
    shift, scale, gate = jnp.split(mod, 3, axis=-1)
    u = layer_norm(x) * (1.0 + scale[:, None, :]) + shift[:, None, :]

    proj = u @ w_in
    (c_q, c_kv, k_rope, g_a, qkv_b, f_logit, g_b, qkv_c, g_c, merge_logit) = split_cols(proj, SPLIT_SIZES)

    q_a = (rms_norm(c_q, q_norm_g) @ w_uq).reshape(bsz, seq, H_MLA, MLA_NOPE + MLA_ROPE)
    q_a = jnp.concatenate([q_a[..., :MLA_NOPE], rope(q_a[..., MLA_NOPE:], positions)], axis=-1)
    kv_a = (rms_norm(c_kv, kv_norm_g) @ w_ukv).reshape(bsz, seq, H_MLA, MLA_NOPE + MLA_V)
    k_pe = jnp.broadcast_to(rope(k_rope[:, :, None, :], positions), (bsz, seq, H_MLA, MLA_ROPE))
    k_a = jnp.concatenate([kv_a[..., :MLA_NOPE], k_pe], axis=-1)
    v_a = kv_a[..., MLA_NOPE:]
    o_a = sweep_query_blocks(mla_block, q_a, k_a, v_a).reshape(bsz, seq, BRANCH_W)

    q_b, k_b, v_b = [t.reshape(bsz, seq, H_FOX, HEAD_DIM) for t in jnp.split(qkv_b, 3, axis=-1)]
    log_f = jax.nn.log_sigmoid(f_logit.astype(jnp.float32) + fox_bias.astype(jnp.float32))
    fcum = jnp.transpose(jnp.cumsum(log_f, axis=1), (0, 2, 1))
    o_b = sweep_query_blocks(lambda qb, kb, vb, q0: fox_block(qb, kb, vb, q0, fcum),
                             q_b, k_b, v_b).reshape(bsz, seq, BRANCH_W)

    q_c, k_c, v_c = [t.reshape(bsz, seq, H_SB, HEAD_DIM) for t in jnp.split(qkv_c, 3, axis=-1)]
    o_c = sweep_query_blocks(stick_breaking_block, q_c, k_c, v_c).reshape(bsz, seq, BRANCH_W)

    ys = jnp.stack([o_a * jax.nn.silu(g_a), o_b * jax.nn.silu(g_b), o_c * jax.nn.silu(g_c)], axis=2)
    branches = jnp.einsum('bsnw,nwd->bsnd', ys, w_branch)
    merge = jax.nn.sigmoid(merge_logit).reshape(bsz, seq, N_BRANCH, D_MODEL)
    merged = jnp.sum(merge * branches, axis=2)
    out = merged @ w_out

    return layer_norm(DEEPNORM_ALPHA * x + gate[:, None, :] * out, ln_g, ln_b)


def setup_inputs(seed: int = 0) -> dict:
    key = jax.random.key(seed)
    ks = jax.random.split(key, 16)
    f32 = jnp.float32
    x = jax.random.normal(ks[0], (BATCH, SEQ, D_MODEL), f32)
    c = jax.random.normal(ks[1], (BATCH, D_MODEL), f32)
    start = jax.random.randint(ks[2], (BATCH, 1), 0, 64, dtype=jnp.int32) * CHUNK
    positions = (start + jnp.arange(SEQ, dtype=jnp.int32)[None, :]).astype(jnp.int32)
    w_ada = jax.random.normal(ks[3], (DEPTH, D_MODEL, 3 * D_MODEL), f32) * (0.5 * D_MODEL ** -0.5)
    b_ada = jax.random.normal(ks[4], (DEPTH, 3 * D_MODEL), f32) * 0.02
    w_in = jax.random.normal(ks[5], (DEPTH, D_MODEL, D_IN), f32) * D_MODEL ** -0.5
    q_norm_g = 1.0 + 0.05 * jax.random.normal(ks[6], (DEPTH, Q_LORA), f32)
    kv_norm_g = 1.0 + 0.05 * jax.random.normal(ks[7], (DEPTH, KV_LORA), f32)
    w_uq = jax.random.normal(ks[8], (DEPTH, Q_LORA, H_MLA * (MLA_NOPE + MLA_ROPE)), f32) * Q_LORA ** -0.5
    w_ukv = jax.random.normal(ks[9], (DEPTH, KV_LORA, H_MLA * (MLA_NOPE + MLA_V)), f32) * KV_LORA ** -0.5
    fox_bias = jax.random.uniform(ks[10], (DEPTH, H_FOX), f32, 1.0, 4.0)
    w_branch = jax.random.normal(ks[11], (DEPTH, N_BRANCH, BRANCH_W, D_MODEL), f32) * (DEEPNORM_BETA * BRANCH_W ** -0.5)
    w_out = jax.random.normal(ks[12], (DEPTH, D_MODEL, D_MODEL), f32) * (DEEPNORM_BETA * D_MODEL ** -0.5)
    ln_g = 1.0 + 0.05 * jax.random.normal(ks[13], (DEPTH, D_MODEL), f32)
    ln_b = 0.02 * jax.random.normal(ks[14], (DEPTH, D_MODEL), f32)
    return {'x': x, 'c': c, 'positions': positions, 'w_ada': w_ada, 'b_ada': b_ada, 'w_in': w_in,
            'q_norm_g': q_norm_g, 'kv_norm_g': kv_norm_g, 'w_uq': w_uq, 'w_ukv': w_ukv,
            'fox_bias': fox_bias, 'w_branch': w_branch, 'w_out': w_out, 'ln_g': ln_g, 'ln_b': ln_b}


def reference(x, c, positions, w_ada, b_ada, w_in, q_norm_g, kv_norm_g, w_uq, w_ukv,
              fox_bias, w_branch, w_out, ln_g, ln_b):
    for l in range(DEPTH):
        x = hybrid_layer(x, c, positions, w_ada[l], b_ada[l], w_in[l], q_norm_g[l], kv_norm_g[l],
                         w_uq[l], w_ukv[l], fox_bias[l], w_branch[l], w_out[l], ln_g[l], ln_b[l])
    return x
```

```python
import math
import numpy as np
import concourse.bass as bass
import concourse.mybir as mybir
from concourse.bass_utils import run_bass_kernel_spmd

F32 = mybir.dt.float32
BF16 = mybir.dt.bfloat16
I32 = mybir.dt.int32
AF = mybir.ActivationFunctionType
ALU = mybir.AluOpType

D = 2048
S = 2048
DEPTH = 4
NB = 16
NR = 4
KC = 16
H = 8
ALPHA = (2 * DEPTH) ** 0.25
LN_EPS = 1e-5
RMS_EPS = 1e-6
NCORES = 4
SEM_CAP = 32000

CH_CQ, CH_CKV, CH_KR, CH_GA = 0, 4, 6, 7
CH_QB, CH_KB, CH_GB = 15, 23, 31
CH_QC, CH_KC, CH_GC = 39, 47, 55
CH_M = 63
NCH = 111


class Buf:
    __slots__ = ("name", "writers", "readers", "prev")

    def __init__(self, name):
        self.name = name
        self.writers = []
        self.readers = []
        self.prev = []


class Op:
    __slots__ = ("eng", "fn", "deps", "is_dma", "sem", "val", "signal", "idx", "wname")


class Prog:
    def __init__(self, nc):
        self.nc = nc
        self.ops = []

    def _add(self, eng, fn, reads, writes, is_dma):
        op = Op()
        op.eng, op.fn, op.is_dma = eng, fn, is_dma
        op.sem, op.val, op.signal = None, 0, False
        op.idx = len(self.ops)
        op.wname = writes[0].name if writes else "none"
        deps = {}
        for b in reads:
            for w in b.writers:
                deps[w.idx] = w
            if not b.writers:
                for a in b.prev:
                    deps[a.idx] = a
        for b in writes:
            if b.readers:
                b.prev = b.readers + b.writers
                b.readers = []
                b.writers = []
            for a in b.prev:
                deps[a.idx] = a
            if b.writers:
                deps[b.writers[-1].idx] = b.writers[-1]
        for b in reads:
            b.readers.append(op)
        for b in writes:
            b.writers.append(op)
        op.deps = list(deps.values())
        self.ops.append(op)
        return op

    def op(self, eng, fn, reads=(), writes=()):
        return self._add(eng, fn, list(reads), list(writes), False)

    def dma(self, eng, out, in_, reads=(), writes=()):
        return self._add(eng, lambda e: e.dma_start(out=out, in_=in_), list(reads), list(writes), True)

    def emit(self, final_ops):
        nc = self.nc

        def skip(d, op):
            return (not d.is_dma) and (not op.is_dma) and d.eng == op.eng == "tensor"

        for op in self.ops:
            for d in op.deps:
                if d.is_dma or skip(d, op):
                    continue
                d.signal = True
        CAP = SEM_CAP
        sem_ctx = []

        def new_sem(name):
            cm = nc.semaphore(name)
            s = cm.__enter__()
            sem_ctx.append(cm)
            return s

        eng_sem, eng_cnt = {}, {}
        dsem = {}
        nsem = 0
        for op in self.ops:
            if op.is_dma:
                ent = dsem.get(op.wname)
                if ent is None or ent[1] + 16 > CAP:
                    ent = [new_sem(f"d{nsem}"), 0]
                    nsem += 1
                    dsem[op.wname] = ent
                ent[1] += 16
                op.sem, op.val = ent[0], ent[1]
            elif op.signal:
                e = op.eng
                if e not in eng_sem or eng_cnt[e] >= CAP:
                    eng_sem[e] = new_sem(f"e{nsem}")
                    nsem += 1
                    eng_cnt[e] = 0
                eng_cnt[e] += 1
                op.sem, op.val = eng_sem[e], eng_cnt[e]
        self.nsem = nsem
        per_eng = {}
        for op in self.ops:
            per_eng.setdefault(op.eng, []).append(op)
        final = list(final_ops)
        with nc.Block() as block:
            def make(engname, oplist):
                def body(e):
                    waited = {}
                    for op in oplist:
                        need = {}
                        for d in op.deps:
                            if skip(d, op):
                                continue
                            k = id(d.sem)
                            if k not in need or need[k][1] < d.val:
                                need[k] = (d.sem, d.val)
                        for k, (s_, v_) in need.items():
                            if waited.get(k, 0) >= v_:
                                continue
                            e.wait_ge(s_, v_)
                            waited[k] = v_
                        ins = op.fn(e)
                        if op.sem is not None:
                            ins.then_inc(op.sem, 16 if op.is_dma else 1)
                    if engname == "sync":
                        for f in final:
                            e.wait_ge(f.sem, f.val)
                return body
            for engname in ("sync", "tensor", "vector", "scalar", "gpsimd"):
                getattr(block, engname)(make(engname, per_eng.get(engname, [])))
        for cm in reversed(sem_ctx):
            cm.__exit__(None, None, None)


def build_program(L, dbg=False, maxphase=99):
    nc = bass.Bass("TRN2", target_bir_lowering=False)

    def din(name, shape, dt=F32):
        return nc.dram_tensor(name, list(shape), dt, kind="ExternalInput").ap()

    def dscr(name, shape, dt):
        return nc.dram_tensor(name, list(shape), dt, kind="Internal").ap()

    x_in = din("x", [S, D])
    cT_in = din("cT", [128, KC])
    pos_in = din("pos", [1, S], I32)
    invf_in = din("invf", [64, 1])
    sgn_in = din("sgn", [64, 1])
    w_ada = din("w_ada", [L, D, 3 * D])
    b_adaT = din("b_adaT", [L, 128, 48])
    gate_b = din("gate_b", [L, 1, D])
    w_ch = din("w_ch", [L, NCH, 128, KC * 128])
    w_wide = din("w_wide", [L, 4, 128, KC * 512])
    w_f = din("w_f", [L, 128, KC * 8])
    gq_in = din("gq", [L, 128, 4])
    gkv_in = din("gkv", [L, 128, 2])
    w_uq = din("w_uq", [L, H, 128, 4 * 256])
    w_uk = din("w_uk", [L, H, 128, 2 * 128])
    w_uv = din("w_uv", [L, 2, 128, 2 * 512])
    fbias = din("fbias", [L, 1, 8])
    w_br = din("w_br", [L, 3, KC, 128, 8 * 128])
    w_out = din("w_out", [L, D, D])
    ln_g = din("ln_g", [L, 1, D])
    ln_b = din("ln_b", [L, 1, D])
    y_out = nc.dram_tensor("y", [S, D], F32, kind="ExternalOutput").ap()

    xs = [dscr("xs0", [S, D], F32), dscr("xs1", [S, D], F32)]
    PT = dscr("PT", [NCH, 128, S], BF16)
    QA = dscr("QA", [H, 3, 128, S], BF16)
    VS = dscr("VS", [3, 2, 128, NB * 512], BF16)
    YS = dscr("YS", [128, 3 * H, S], BF16)
    GR = dscr("GR", [128, D], F32)
    ZS = dscr("ZS", [S, D], F32)

    P = Prog(nc)
    ctx = []

    def sb(name, shape, dt):
        cm = nc.sbuf_tensor(name, list(shape), dt)
        t = cm.__enter__()
        ctx.append(cm)
        return t

    def psum(name, shape, dt):
        cm = nc.psum_tensor(name, list(shape), dt)
        t = cm.__enter__()
        ctx.append(cm)
        return t

    uT = sb("uT", [128, KC * S], BF16)
    arena = sb("arena", [128, 32768], BF16)
    wst = [sb(f"wst{i}", [128, 2048], F32) for i in range(2)]
    wbf = [sb(f"wbf{i}", [128, 2048], BF16) for i in range(5)]
    cos2 = sb("cos2", [64, S], BF16)
    sin2 = sb("sin2", [64, S], BF16)
    kpeT = sb("kpeT", [64, S], BF16)
    small = sb("small", [128, 1024], F32)
    cst = sb("cst", [128, 1152], BF16)
    cstf = sb("cstf", [128, 768], F32)
    dgt = sb("dgt", [128, 256], F32)
    a32 = arena[:, :].bitcast(F32)
    ai32 = arena[:, :].bitcast(I32)

    B = {}

    def buf(name):
        if name not in B:
            B[name] = Buf(name)
        return B[name]

    ps = [psum(f"ps{i}", [128, 512], F32) for i in range(8)]
    psb = [buf(f"ps{i}") for i in range(8)]

    ident = cst[:, 0:128]
    m_fox = cst[:, 128:256]
    m_sb = cst[:, 256:384]
    m_mla = cst[:, 384:512]
    negU = cst[:, 512:640]
    negones = cst[:, 640:768]
    ones_b = cst[:, 768:896]
    cT_b = cst[:, 896:912]
    zeros_b = cst[:, 1024:1152]
    onesf = cstf[:, 0:128]
    negtri = cstf[:, 128:256]
    sel127 = cstf[:, 256:384]
    negonesf = cstf[:, 384:512]
    identf = cstf[:, 512:640]
    mnegf = cstf[:, 640:768]
    bconst = buf("const")

    modT = small[:, 0:32]
    sc1T = small[:, 48:64]
    stats = small[:, 64:88]
    cT_f = small[:, 96:112]
    gq_s = small[:, 112:116]
    gkv_s = small[:, 116:118]
    fb_s = small[:, 120:128]
    badaT = small[:, 128:176]
    invf = small[0:64, 176:177]
    sgn = small[0:64, 177:178]
    gateT = small[:, 178:194]
    Fcum = small[:, 192:320]
    Cend = small[:, 320:448]
    SPf = small[:, 448:576]
    fbt = small[:, 576:832]
    mvs = small[:, 832:896]
    rkv = small[:, 896:912]

    phase_bufs = []

    def abuf(name):
        b = buf(name)
        if b not in phase_bufs:
            phase_bufs.append(b)
        return b

    def fence():
        nonlocal phase_bufs
        if not phase_bufs:
            return
        old = phase_bufs
        f = P.op("vector", lambda e: e.memset(small[:, 1020:1021], 0.0), reads=old, writes=old)
        for b in old:
            b.prev = [f]
            b.writers = []
            b.readers = []
        for name in list(B.keys()):
            if B[name] in old:
                del B[name]
        phase_bufs = []
        return f

    fence_op = [None]

    def pbuf(name):
        if name not in B:
            b = abuf(name)
            if fence_op[0] is not None:
                b.prev = [fence_op[0]]
        return B[name]

    def do_fence():
        f = fence()
        if f is not None:
            fence_op[0] = f

    cnt = {"wst": 0, "wbf": 0, "q": 0, "ps": 0, "ev": 0}
    wst_b = [buf(f"wst{i}") for i in range(2)]
    wbf_b = [buf(f"wbf{i}") for i in range(5)]

    def dq():
        cnt["q"] += 1
        return "sync"

    def load_cast(src_ap, n_el, scale_ap=None, nk=1, cast_eng="vector", extra_reads=()):
        i = cnt["wst"] % 2
        cnt["wst"] += 1
        j = cnt["wbf"] % 5
        cnt["wbf"] += 1
        st, stb = wst[i], wst_b[i]
        wt, wtb = wbf[j], wbf_b[j]
        P.dma("sync", st[:, 0:n_el], src_ap, writes=[stb])
        if scale_ap is None:
            P.op(cast_eng, lambda e: e.tensor_copy(out=wt[:, 0:n_el], in_=st[:, 0:n_el]), reads=[stb], writes=[wtb])
        else:
            w = n_el // nk
            for kk in range(nk):
                P.op(cast_eng, lambda e, kk=kk: e.tensor_scalar(out=wt[:, kk * w:(kk + 1) * w], in0=st[:, kk * w:(kk + 1) * w],
                                                               scalar1=scale_ap[:, kk:kk + 1], scalar2=1.0,
                                                               op0=ALU.mult, op1=ALU.mult),
                     reads=[stb] + list(extra_reads), writes=[wtb])
        return wt, wtb

    def nps(n=6):
        i = cnt["ps"] % n
        cnt["ps"] += 1
        return i

    def ev_eng():
        cnt["ev"] += 1
        return "vector" if cnt["ev"] % 2 else "scalar"

    def copy_scaled(eng, out, in_, scale=None):
        if eng == "scalar":
            if scale is None:
                return lambda e: e.copy(out=out, in_=in_)
            return lambda e: e.mul(out=out, in_=in_, mul=float(scale))
        if scale is None:
            return lambda e: e.tensor_copy(out=out, in_=in_)
        return lambda e: e.tensor_scalar(out=out, in0=in_, scalar1=float(scale), scalar2=1.0, op0=ALU.mult, op1=ALU.mult)

    g = "gpsimd"
    P.op(g, lambda e: e.memset(cst[:, :], 0.0), writes=[bconst])
    P.op(g, lambda e: e.memset(cstf[:, :], 0.0), writes=[bconst])
    P.op(g, lambda e: e.memset(ones_b, 1.0), writes=[bconst])
    P.op(g, lambda e: e.memset(negones, -1.0), writes=[bconst])
    P.op(g, lambda e: e.memset(onesf, 1.0), writes=[bconst])
    P.op(g, lambda e: e.memset(negonesf, -1.0), writes=[bconst])
    bc2 = buf("const2")
    P.op(g, lambda e: e.affine_select(out=ident, in_=ident, pattern=[[-1, 128]], compare_op=ALU.not_equal,
                                      fill=1.0, base=0, channel_multiplier=1), reads=[bconst], writes=[bc2])
    P.op(g, lambda e: e.affine_select(out=m_fox, in_=ones_b, pattern=[[1, 128]], compare_op=ALU.is_ge,
                                      fill=0.0, base=0, channel_multiplier=-1), reads=[bconst], writes=[bc2])
    P.op(g, lambda e: e.affine_select(out=m_sb, in_=ones_b, pattern=[[1, 128]], compare_op=ALU.is_gt,
                                      fill=0.0, base=0, channel_multiplier=-1), reads=[bconst], writes=[bc2])
    P.op(g, lambda e: e.memset(m_mla, 1.0), reads=[bconst], writes=[bc2])
    bc3 = buf("const3")
    P.op(g, lambda e: e.memset(cst[64:128, 384:448], 0.0), reads=[bc2], writes=[bc3])
    P.op(g, lambda e: e.affine_select(out=negU, in_=negones, pattern=[[-1, 128]], compare_op=ALU.is_gt,
                                      fill=0.0, base=0, channel_multiplier=1), reads=[bconst], writes=[bc2])
    P.op(g, lambda e: e.affine_select(out=negtri, in_=negonesf, pattern=[[1, 128]], compare_op=ALU.is_ge,
                                      fill=0.0, base=0, channel_multiplier=-1), reads=[bconst], writes=[bc2])
    P.op(g, lambda e: e.affine_select(out=sel127, in_=onesf, pattern=[[0, 128]], compare_op=ALU.is_equal,
                                      fill=0.0, base=-127, channel_multiplier=1), reads=[bconst], writes=[bc2])
    P.op(g, lambda e: e.affine_select(out=identf, in_=identf, pattern=[[-1, 128]], compare_op=ALU.not_equal,
                                      fill=1.0, base=0, channel_multiplier=1), reads=[bconst], writes=[bc2])
    P.op(g, lambda e: e.affine_select(out=mnegf, in_=mnegf, pattern=[[1, 128]], compare_op=ALU.is_ge,
                                      fill=-30000.0, base=0, channel_multiplier=-1), reads=[bconst], writes=[bc2])
    CONST = [bconst, bc2, bc3]
    bsm = buf("small_c")
    P.dma("sync", cT_f, cT_in[:, :], writes=[bsm])
    P.dma("sync", invf, invf_in[:, :], writes=[bsm])
    P.dma("sync", sgn, sgn_in[:, :], writes=[bsm])
    bc4 = buf("const4")
    P.op("vector", lambda e: e.tensor_copy(out=cT_b, in_=cT_f), reads=[bsm, bconst], writes=[bc4])
    CONST.append(bc4)
    posi = ai32[0:64, 0:S]
    r1 = a32[0:64, S:2 * S]
    ri = ai32[0:64, 2 * S:3 * S]
    rf = a32[0:64, 3 * S:4 * S]
    fr = a32[0:64, 4 * S:5 * S]
    bt = [pbuf(f"rope_t{i}") for i in range(5)]
    P.dma("sync", posi, pos_in.partition_broadcast(64), writes=[bt[0]])
    v = "vector"
    P.op(v, lambda e: e.tensor_copy(out=r1, in_=posi), reads=[bt[0]], writes=[bt[1]])
    P.op(v, lambda e: e.tensor_scalar(out=r1, in0=r1, scalar1=invf, scalar2=float(1.0 / (2 * math.pi)),
                                      op0=ALU.mult, op1=ALU.mult), reads=[bt[1], bsm], writes=[bt[1]])
    bc5 = buf("const5")
    for which, dst in ((0, sin2), (1, cos2)):
        if which == 1:
            P.op(v, lambda e: e.tensor_scalar(out=r1, in0=r1, scalar1=0.25, scalar2=1.0, op0=ALU.add, op1=ALU.mult),
                 reads=[bt[1]], writes=[bt[1]])
        P.op(v, lambda e: e.tensor_copy(out=ri, in_=r1), reads=[bt[1]], writes=[bt[2]])
        P.op(v, lambda e: e.tensor_copy(out=rf, in_=ri), reads=[bt[2]], writes=[bt[3]])
        P.op(v, lambda e: e.tensor_tensor(out=fr, in0=r1, in1=rf, op=ALU.subtract), reads=[bt[1], bt[3]], writes=[bt[4]])
        P.op("scalar", lambda e: e.activation(out=fr, in_=fr, func=AF.Sin, scale=float(2 * math.pi)),
             reads=[bt[4]], writes=[bt[4]])
        if which == 0:
            P.op(v, lambda e, dst=dst: e.tensor_scalar(out=dst[:, :], in0=fr, scalar1=sgn, scalar2=1.0,
                                                      op0=ALU.mult, op1=ALU.mult), reads=[bt[4], bsm], writes=[bc5])
        else:
            P.op(v, lambda e, dst=dst: e.tensor_copy(out=dst[:, :], in_=fr), reads=[bt[4]], writes=[bc5])
    CONST.append(bc5)
    do_fence()

    x_src, x_srcb = x_in, buf("x_in")
    final_ops = []
    dbg_outs = {}

    for l in range(L):
        last = (l == L - 1)
        x_dst = y_out if last else xs[l % 2]
        x_dstb = buf("y_out") if last else buf(f"xs{l % 2}")
        bsl = buf("small_l")

        P.dma("sync", badaT, b_adaT[l], writes=[bsl])
        P.dma("sync", gq_s, gq_in[l], writes=[bsl])
        P.dma("sync", gkv_s, gkv_in[l], writes=[bsl])
        P.dma("sync", fb_s, fbias[l].partition_broadcast(128), writes=[bsl])
        cB = arena[:, 30720:32768]
        bcB = pbuf("cB")
        for kc in range(KC):
            P.op("vector", lambda e, kc=kc: e.tensor_copy(out=cB[:, kc * 128:(kc + 1) * 128],
                                                          in_=cT_f[:, kc:kc + 1].to_broadcast([128, 128])),
                 reads=[bsm], writes=[bcB])
        slabs = [a32[:, 0:6144], a32[:, 6144:12288]]
        bslab = [pbuf("ada_slab0"), pbuf("ada_slab1")]
        slb = arena[:, 24576:30720]
        bb_ = pbuf("ada_slb")
        for kc in range(KC):
            slab, bs_ = slabs[kc % 2], bslab[kc % 2]
            P.dma("sync", slab, w_ada[l, kc * 128:(kc + 1) * 128, :], writes=[bs_])
            P.op("gpsimd", lambda e, slab=slab: e.tensor_copy(out=slb[:, 0:1536], in_=slab[:, 0:1536]), reads=[bs_], writes=[bb_])
            P.op("vector", lambda e, slab=slab: e.tensor_copy(out=slb[:, 1536:6144], in_=slab[:, 1536:6144]), reads=[bs_], writes=[bb_])

            def mm(e, kc=kc):
                ins = None
                for j in range(32):
                    ins = e.matmul(ps[0][:, j:j + 1], lhsT=slb[:, j * 128:(j + 1) * 128], rhs=cT_b[:, kc:kc + 1],
                                   start=True, stop=True)
                for r in range(4):
                    ins = e.matmul(ps[1 + r][:, :], lhsT=cB[:, kc * 128:(kc + 1) * 128],
                                   rhs=slb[:, 4096 + r * 512:4096 + (r + 1) * 512],
                                   start=(kc == 0), stop=(kc == KC - 1))
                return ins
            P.op("tensor", mm, reads=[bb_, bcB] + CONST, writes=psb[0:5])
            if kc == 0:
                P.op("vector", lambda e: e.tensor_copy(out=modT, in_=ps[0][:, 0:32]), reads=[psb[0]], writes=[bsl])
            else:
                P.op("vector", lambda e: e.tensor_tensor(out=modT, in0=modT, in1=ps[0][:, 0:32], op=ALU.add), reads=[psb[0], bsl], writes=[bsl])
        P.op("vector", lambda e: e.tensor_tensor(out=modT, in0=modT, in1=badaT[:, 0:32], op=ALU.add),
             reads=[bsl], writes=[bsl])
        P.op("vector", lambda e: e.tensor_scalar(out=sc1T, in0=modT[:, 16:32], scalar1=1.0, scalar2=1.0, op0=ALU.add, op1=ALU.mult),
             reads=[bsl], writes=[bsl])
        grow = a32[:, 0:2048]
        bgr = bslab[0]
        gbr = wst[0]
        P.dma("sync", gbr[:, 0:2048], gate_b[l].partition_broadcast(128), writes=[wst_b[0]])
        for r in range(4):
            P.op("vector", lambda e, r=r: e.tensor_tensor(out=grow[:, r * 512:(r + 1) * 512], in0=ps[1 + r][:, :],
                                                         in1=gbr[:, r * 512:(r + 1) * 512], op=ALU.add),
                 reads=[psb[1 + r], wst_b[0]], writes=[bgr])
        bGR = buf("GR")
        P.dma("scalar", GR[:, :], grow, reads=[bgr], writes=[bGR])
        do_fence()

        if maxphase <= 0:
            break
        buT = [buf(f"uT{r}") for r in range(NR)]
        for tb in range(NB):
            sl = tb % 2
            xb = a32[:, sl * 2048:(sl + 1) * 2048]
            xn = arena[:, 8192 + sl * 2048:8192 + (sl + 1) * 2048]
            bx, bxn, bst = pbuf(f"p1x{sl}"), pbuf(f"p1xn{sl}"), pbuf(f"p1st{sl}")
            P.dma(dq(), xb, x_src[tb * 128:(tb + 1) * 128, :], reads=[x_srcb], writes=[bx])
            stats = small[:, 912 + sl * 24:912 + (sl + 1) * 24]
            mv = mvs[:, sl * 4:sl * 4 + 2]
            rs = mvs[:, sl * 4 + 2:sl * 4 + 3]
            for q4 in range(4):
                P.op(v, lambda e, q4=q4, xb=xb, stats=stats: e.bn_stats(out=stats[:, q4 * 6:(q4 + 1) * 6], in_=xb[:, q4 * 512:(q4 + 1) * 512]),
                     reads=[bx], writes=[bst])
            P.op(v, lambda e, mv=mv, stats=stats: e.bn_aggr(out=mv, in_=stats), reads=[bst], writes=[bst])
            P.op(v, lambda e, mv=mv, rs=rs: e.tensor_scalar(out=rs, in0=mv[:, 1:2], scalar1=LN_EPS, scalar2=1.0, op0=ALU.add, op1=ALU.mult),
                 reads=[bst], writes=[bst])
            P.op("scalar", lambda e, rs=rs: e.activation(out=rs, in_=rs, func=AF.Sqrt), reads=[bst], writes=[bst])
            P.op(v, lambda e, rs=rs: e.reciprocal(out=rs, in_=rs), reads=[bst], writes=[bst])
            P.op(v, lambda e, xb=xb, xn=xn, mv=mv, rs=rs: e.tensor_scalar(out=xn, in0=xb, scalar1=mv[:, 0:1], scalar2=rs,
                                                                         op0=ALU.subtract, op1=ALU.mult),
                 reads=[bx, bst], writes=[bxn])
            for half in range(2):
                pi_ = nps(4)
                pst = ps[pi_][:, :].bitcast(BF16)

                def tr(e, half=half, xn=xn, pst=pst):
                    ins = None
                    for k8 in range(8):
                        kc = half * 8 + k8
                        ins = e.transpose(pst[:, k8 * 128:(k8 + 1) * 128], xn[:, kc * 128:(kc + 1) * 128], ident)
                    return ins
                P.op("tensor", tr, reads=[bxn] + CONST, writes=[psb[pi_]])
                for k8 in range(8):
                    kc = half * 8 + k8
                    dst = uT[:, kc * S + tb * 128: kc * S + (tb + 1) * 128]
                    src = pst[:, k8 * 128:(k8 + 1) * 128]
                    if k8 % 2 == 0:
                        P.op("scalar", lambda e, dst=dst, src=src, kc=kc: e.activation(out=dst, in_=src, func=AF.Identity,
                                                                                      bias=modT[:, kc:kc + 1], scale=sc1T[:, kc:kc + 1]),
                             reads=[psb[pi_], bsl], writes=[buT[tb // 4]])
                    else:
                        P.op(v, lambda e, dst=dst, src=src, kc=kc: e.tensor_scalar(out=dst, in0=src, scalar1=sc1T[:, kc:kc + 1],
                                                                                  scalar2=modT[:, kc:kc + 1], op0=ALU.mult, op1=ALU.add),
                             reads=[psb[pi_], bsl], writes=[buT[tb // 4]])
        do_fence()

        if maxphase <= 1:
            break
        bPT, bQA, bVS, bYS = buf("PT"), buf("QA"), buf("VS"), buf("YS")
        rowst = [arena[:, i * 2048:(i + 1) * 2048] for i in range(3)]
        rowb = [pbuf(f"row{i}") for i in range(3)]
        cqn = arena[:, 6144:14336]
        ckvn = arena[:, 14336:18432]
        bcq, bckv = pbuf("cqn"), pbuf("ckvn")
        sqt = [arena[:, 18432 + i * 512:18432 + (i + 1) * 512] for i in range(6)]
        bsq = pbuf("sq")
        rr = a32[:, 10752:11264]
        brr = pbuf("rr")
        vst = [arena[:, 22528 + i * 512:22528 + (i + 1) * 512] for i in range(2)]
        bvst = [pbuf(f"vst{i}") for i in range(2)]
        wide = arena[:, 23552:31744]
        bwide = pbuf("wide")
        t1 = a32[0:64, 15872:16384]
        bt1 = pbuf("t1")
        cnt_row = [0]

        def proj_rows(wt, wtb, width, evac, extra_reads=(), kcn=KC, src=None, srcb=None, wstride=128, woff=0):
            for r in range(NR):
                pi_ = nps(6)

                def mm(e, r=r, pi_=pi_):
                    ins = None
                    for kc in range(kcn):
                        if src is None:
                            rhs = uT[:, kc * S + r * 512: kc * S + (r + 1) * 512]
                        else:
                            rhs = src[:, kc * S + r * 512: kc * S + (r + 1) * 512]
                        ins = e.matmul(ps[pi_][0:width, :], lhsT=wt[:, kc * wstride + woff: kc * wstride + woff + width],
                                       rhs=rhs, start=(kc == 0), stop=(kc == kcn - 1))
                    return ins
                rd = [wtb] + (list(buT) if src is None else [srcb]) + list(extra_reads)
                P.op("tensor", mm, reads=rd, writes=[psb[pi_]])
                evac(r, pi_)

        def simple_chunk(chunk, kind, dst_ap_fn, dstb):
            wt, wtb = load_cast(w_ch[l, chunk], 2048)
            i = cnt_row[0] % 3
            cnt_row[0] += 1
            row, rb = rowst[i], rowb[i]

            def evac(r, pi_):
                out = row[:, r * 512:(r + 1) * 512]
                if kind == "q":
                    eng = ev_eng()
                    P.op(eng, copy_scaled(eng, out, ps[pi_][:, :], 1.0 / math.sqrt(128.0)), reads=[psb[pi_]], writes=[rb])
                elif kind == "k":
                    eng = ev_eng()
                    P.op(eng, copy_scaled(eng, out, ps[pi_][:, :]), reads=[psb[pi_]], writes=[rb])
                elif kind == "g":
                    P.op("scalar", lambda e: e.activation(out=out, in_=ps[pi_][:, :], func=AF.Silu), reads=[psb[pi_]], writes=[rb])
                elif kind == "m":
                    P.op("scalar", lambda e: e.activation(out=out, in_=ps[pi_][:, :], func=AF.Sigmoid), reads=[psb[pi_]], writes=[rb])
            proj_rows(wt, wtb, 128, evac)
            P.dma("scalar", dst_ap_fn(), row, reads=[rb], writes=[dstb])

        for (ch0, nchk, dstT, dstb, ncols) in ((CH_CQ, 4, cqn, bcq, 512.0), (CH_CKV, 2, ckvn, bckv, 256.0)):
            wts = [load_cast(w_ch[l, ch0 + i], 2048) for i in range(nchk)]
            for r in range(NR):
                for i in range(nchk):
                    wt, wtb = wts[i]

                    def mm(e, r=r, i=i, wt=wt):
                        ins = None
                        for kc in range(KC):
                            ins = e.matmul(ps[i][:, :], lhsT=wt[:, kc * 128:(kc + 1) * 128],
                                           rhs=uT[:, kc * S + r * 512: kc * S + (r + 1) * 512],
                                           start=(kc == 0), stop=(kc == KC - 1))
                        return ins
                    P.op("tensor", mm, reads=[wtb] + list(buT), writes=[psb[i]])
                    P.op("scalar", lambda e, i=i: e.activation(out=sqt[i], in_=ps[i][:, :], func=AF.Square),
                         reads=[psb[i]], writes=[bsq])

                def mss(e, nchk=nchk):
                    ins = None
                    for i in range(nchk):
                        ins = e.matmul(ps[4][:, :], lhsT=ones_b, rhs=sqt[i], start=(i == 0), stop=(i == nchk - 1))
                    return ins
                P.op("tensor", mss, reads=[bsq] + CONST, writes=[psb[4]])
                P.op(v, lambda e, ncols=ncols: e.tensor_scalar(out=rr, in0=ps[4][:, :], scalar1=1.0 / ncols, scalar2=RMS_EPS,
                                                               op0=ALU.mult, op1=ALU.add), reads=[psb[4]], writes=[brr])
                P.op("scalar", lambda e: e.activation(out=rr, in_=rr, func=AF.Sqrt), reads=[brr], writes=[brr])
                P.op(v, lambda e: e.reciprocal(out=rr, in_=rr), reads=[brr], writes=[brr])
                for i in range(nchk):
                    P.op(v, lambda e, i=i, r=r, dstT=dstT: e.tensor_tensor(out=dstT[:, i * S + r * 512: i * S + (r + 1) * 512],
                                                                          in0=ps[i][:, :], in1=rr, op=ALU.mult),
                         reads=[psb[i], brr], writes=[dstb])
        bkpe = buf("kpeT")
        wt, wtb = load_cast(w_ch[l, CH_KR], 2048)
        for r in range(NR):
            def mm(e, r=r, wt=wt):
                ins = None
                for half in range(2):
                    for kc in range(KC):
                        ins = e.matmul(ps[5 + half][0:64, :], lhsT=wt[:, kc * 128 + half * 64: kc * 128 + half * 64 + 64],
                                       rhs=uT[:, kc * S + r * 512: kc * S + (r + 1) * 512],
                                       start=(kc == 0), stop=(kc == KC - 1))
                return ins
            P.op("tensor", mm, reads=[wtb] + list(buT), writes=[psb[5], psb[6]])
            cs = slice(r * 512, (r + 1) * 512)
            P.op(v, lambda e, cs=cs: e.tensor_tensor(out=t1, in0=ps[5][0:64, :], in1=cos2[:, cs], op=ALU.mult),
                 reads=[psb[5]] + CONST, writes=[bt1])
            P.op(v, lambda e, cs=cs: e.tensor_tensor(out=rr[0:64, :], in0=ps[6][0:64, :], in1=sin2[:, cs], op=ALU.mult),
                 reads=[psb[6]] + CONST, writes=[brr])
            P.op(v, lambda e, cs=cs: e.tensor_tensor(out=kpeT[:, cs], in0=t1, in1=rr[0:64, :], op=ALU.add),
                 reads=[bt1, brr], writes=[bkpe])
        SC_A = 1.0 / math.sqrt(192.0)
        for h in range(H):
            wq, wqb = load_cast(w_uq[l, h], 1024, scale_ap=gq_s, nk=4, extra_reads=[bsl])
            i = cnt_row[0] % 3
            cnt_row[0] += 1
            row, rb = rowst[i], rowb[i]

            def evq(r, pi_, row=row, rb=rb):
                eng = ev_eng()
                P.op(eng, copy_scaled(eng, row[:, r * 512:(r + 1) * 512], ps[pi_][:, :], SC_A), reads=[psb[pi_]], writes=[rb])
            proj_rows(wq, wqb, 128, evq, kcn=4, src=cqn, srcb=bcq, wstride=256, woff=0)
            P.dma("scalar", QA[h, 0], row, reads=[rb], writes=[bQA])
            i = cnt_row[0] % 3
            cnt_row[0] += 1
            row, rb = rowst[i], rowb[i]
            for r in range(NR):
                def mm(e, r=r, wq=wq):
                    ins = None
                    for half in range(2):
                        for kc in range(4):
                            ins = e.matmul(ps[5 + half][0:64, :], lhsT=wq[:, kc * 256 + 128 + half * 64: kc * 256 + 192 + half * 64],
                                           rhs=cqn[:, kc * S + r * 512: kc * S + (r + 1) * 512],
                                           start=(kc == 0), stop=(kc == 3))
                    return ins
                P.op("tensor", mm, reads=[wqb, bcq], writes=[psb[5], psb[6]])
                cs = slice(r * 512, (r + 1) * 512)
                P.op(v, lambda e, cs=cs: e.tensor_tensor(out=t1, in0=ps[5][0:64, :], in1=cos2[:, cs], op=ALU.mult),
                     reads=[psb[5]] + CONST, writes=[bt1])
                P.op(v, lambda e, cs=cs: e.tensor_tensor(out=rr[0:64, :], in0=ps[6][0:64, :], in1=sin2[:, cs], op=ALU.mult),
                     reads=[psb[6]] + CONST, writes=[brr])
                P.op(v, lambda e: e.tensor_tensor(out=t1, in0=t1, in1=rr[0:64, :], op=ALU.add),
                     reads=[bt1, brr], writes=[bt1])
                P.op(v, lambda e, cs=cs, row=row: e.tensor_scalar(out=row[0:64, cs], in0=t1, scalar1=SC_A, scalar2=1.0,
                                                                 op0=ALU.mult, op1=ALU.mult),
                     reads=[bt1], writes=[rb])
            P.dma("scalar", QA[h, 1, 0:64, :], row[0:64, :], reads=[rb], writes=[bQA])
            wk, wkb = load_cast(w_uk[l, h], 256, scale_ap=gkv_s, nk=2, extra_reads=[bsl])
            i = cnt_row[0] % 3
            cnt_row[0] += 1
            row, rb = rowst[i], rowb[i]

            def evk(r, pi_, row=row, rb=rb):
                eng = ev_eng()
                P.op(eng, copy_scaled(eng, row[:, r * 512:(r + 1) * 512], ps[pi_][:, :]), reads=[psb[pi_]], writes=[rb])
            proj_rows(wk, wkb, 128, evk, kcn=2, src=ckvn, srcb=bckv, wstride=128, woff=0)
            P.dma("scalar", QA[h, 2], row, reads=[rb], writes=[bQA])
        for gidx in range(2):
            wv, wvb = load_cast(w_uv[l, gidx], 1024, scale_ap=gkv_s, nk=2, extra_reads=[bsl])
            for tb in range(NB):
                pi_ = nps(6)

                def mm(e, tb=tb, pi_=pi_, wv=wv):
                    ins = None
                    for kc in range(2):
                        ins = e.matmul(ps[pi_][:, :], lhsT=ckvn[:, kc * S + tb * 128: kc * S + (tb + 1) * 128],
                                       rhs=wv[:, kc * 512:(kc + 1) * 512], start=(kc == 0), stop=(kc == 1))
                    return ins
                P.op("tensor", mm, reads=[wvb, bckv], writes=[psb[pi_]])
                j = tb % 2
                eng = ev_eng()
                P.op(eng, copy_scaled(eng, vst[j], ps[pi_][:, :]), reads=[psb[pi_]], writes=[bvst[j]])
                P.dma("scalar", VS[0, gidx, :, tb * 512:(tb + 1) * 512], vst[j], reads=[bvst[j]], writes=[bVS])
        for h in range(H):
            simple_chunk(CH_GA + h, "g", lambda h=h: PT[CH_GA + h], bPT)
        for (cq_, ck_, cg_) in ((CH_QB, CH_KB, CH_GB), (CH_QC, CH_KC, CH_GC)):
            for h in range(H):
                simple_chunk(cq_ + h, "q", lambda c=cq_ + h: PT[c], bPT)
                simple_chunk(ck_ + h, "k", lambda c=ck_ + h: PT[c], bPT)
            for h in range(H):
                simple_chunk(cg_ + h, "g", lambda c=cg_ + h: PT[c], bPT)
        for n in (1, 2):
            for gidx in range(2):
                wi = (n - 1) * 2 + gidx
                for piece in range(4):
                    i = cnt["wst"] % 2
                    cnt["wst"] += 1
                    P.dma("sync", wst[i][:, :], w_wide[l, wi, :, piece * 2048:(piece + 1) * 2048], writes=[wst_b[i]])
                    P.op("vector", lambda e, i=i, piece=piece: e.tensor_copy(out=wide[:, piece * 2048:(piece + 1) * 2048], in_=wst[i][:, :]),
                         reads=[wst_b[i]], writes=[bwide])
                for tb in range(NB):
                    pi_ = nps(6)

                    def mm(e, tb=tb, pi_=pi_):
                        ins = None
                        for kc in range(KC):
                            ins = e.matmul(ps[pi_][:, :], lhsT=uT[:, kc * S + tb * 128: kc * S + (tb + 1) * 128],
                                           rhs=wide[:, kc * 512:(kc + 1) * 512], start=(kc == 0), stop=(kc == KC - 1))
                        return ins
                    P.op("tensor", mm, reads=[bwide] + list(buT), writes=[psb[pi_]])
                    j = tb % 2
                    eng = ev_eng()
                    P.op(eng, copy_scaled(eng, vst[j], ps[pi_][:, :]), reads=[psb[pi_]], writes=[bvst[j]])
                    P.dma("scalar", VS[n, gidx, :, tb * 512:(tb + 1) * 512], vst[j], reads=[bvst[j]], writes=[bVS])
        wf, wfb = load_cast(w_f[l], 128)
        bfc = buf("fcum")
        for tb in range(NB):
            def mm(e, tb=tb, wf=wf):
                ins = None
                for kc in range(KC):
                    ins = e.matmul(ps[6][:, tb * 8:(tb + 1) * 8], lhsT=uT[:, kc * S + tb * 128: kc * S + (tb + 1) * 128],
                                   rhs=wf[:, kc * 8:(kc + 1) * 8], start=(kc == 0), stop=(kc == KC - 1))
                return ins
            P.op("tensor", mm, reads=[wfb] + list(buT), writes=[psb[6]])
        for tb in range(NB):
            P.op(v, lambda e, tb=tb: e.tensor_tensor(out=SPf[:, tb * 8:(tb + 1) * 8], in0=ps[6][:, tb * 8:(tb + 1) * 8],
                                                    in1=fb_s, op=ALU.add), reads=[psb[6], bsl], writes=[bfc])
        P.op("scalar", lambda e: e.activation(out=SPf, in_=SPf, func=AF.Exp, scale=-1.0), reads=[bfc], writes=[bfc])
        P.op("scalar", lambda e: e.activation(out=SPf, in_=SPf, func=AF.Ln, bias=1.0, scale=1.0), reads=[bfc], writes=[bfc])

        def mmc(e):
            ins = None
            for tb in range(NB):
                for t2 in range(tb + 1):
                    ins = e.matmul(ps[7][:, tb * 8:(tb + 1) * 8], lhsT=(negtri if t2 == tb else negonesf),
                                   rhs=SPf[:, t2 * 8:(t2 + 1) * 8], start=(t2 == 0), stop=(t2 == tb))
            return ins
        P.op("tensor", mmc, reads=[bfc] + CONST, writes=[psb[7]])
        bfc2 = buf("fcum2")
        P.op(v, lambda e: e.tensor_copy(out=Fcum, in_=ps[7][:, 0:128]), reads=[psb[7]], writes=[bfc2])
        bfc3 = buf("fcum3")
        P.op(v, lambda e: e.tensor_scalar(out=Cend, in0=Fcum, scalar1=-1.0, scalar2=1.0, op0=ALU.mult, op1=ALU.mult),
             reads=[bfc2], writes=[bfc3])
        for n in range(3):
            for dc in range(KC):
                c = CH_M + n * KC + dc
                simple_chunk(c, "m", lambda c=c: PT[c], bPT)
        do_fence()

        if maxphase <= 2:
            break
        hb = {}
        for i in range(2):
            hb[i] = dict(q=arena[:, (i * 3 + 0) * 2048:(i * 3 + 1) * 2048], k=arena[:, (i * 3 + 1) * 2048:(i * 3 + 2) * 2048],
                         g=arena[:, (i * 3 + 2) * 2048:(i * 3 + 3) * 2048], qp=arena[0:64, 12288 + i * 2048:12288 + (i + 1) * 2048],
                         bq=pbuf(f"hq{i}"), bk=pbuf(f"hk{i}"), bg=pbuf(f"hg{i}"), bqp=pbuf(f"hqp{i}"))
        Vg = arena[:, 16384:24576]
        bVg = pbuf("Vg")
        W0 = 24576
        pT = [[arena[:, W0 + (s_ * 2 + i) * 512:W0 + (s_ * 2 + i + 1) * 512] for i in range(2)] for s_ in range(2)]
        bpT = [[pbuf(f"pT{s_}{i}") for i in range(2)] for s_ in range(2)]
        spt = [[a32[:, (W0 + 2048) // 2 + (s_ * 2 + i) * 512:(W0 + 2048) // 2 + (s_ * 2 + i + 1) * 512] for i in range(2)] for s_ in range(2)]
        bspt = [[pbuf(f"spt{s_}{i}") for i in range(2)] for s_ in range(2)]
        yst = [arena[:, W0 + 6144 + i * 512:W0 + 6144 + (i + 1) * 512] for i in range(2)]
        byst = [pbuf(f"yst{i}") for i in range(2)]
        recs = [a32[:, (W0 + 7168) // 2:(W0 + 7168) // 2 + 512], wst[0][:, 0:512]]
        brecs = [pbuf("rec0"), pbuf("rec1")]
        Rfs = recs
        bRfs = brecs
        Lns = [[wbf[0][:, (s_ * 2 + i) * 512:(s_ * 2 + i + 1) * 512] for i in range(2)] for s_ in range(2)]
        bLns = [[pbuf(f"Ln{s_}{i}") for i in range(2)] for s_ in range(2)]
        Rbv = [[arena[:, 12288 + (s_ * 3 + i) * 512:12288 + (s_ * 3 + i + 1) * 512] for i in range(3)] for s_ in range(2)]
        bRbv = [[pbuf(f"Rb{s_}{i}") for i in range(3)] for s_ in range(2)]
        Frow = a32[:, 6144:8192]
        bFrow = pbuf("Frow")
        bdg = [pbuf("dg0"), pbuf("dg1")]
        grp0 = [wst_b[0], wbf_b[0], brecs[1]] + bLns[0] + bLns[1]
        P.op(v, lambda e: e.memset(small[:, 1019:1020], 0.0), reads=grp0, writes=grp0)
        hcount = 0
        ycount = [0]
        for n in range(3):
            if n == 1:
                grp = [hb[0]["bqp"], hb[1]["bqp"], bFrow]
                P.op(v, lambda e: e.memset(small[:, 1021:1022], 0.0), reads=grp, writes=grp)
            if n == 2:
                grp = [bFrow] + bRbv[0] + bRbv[1]
                P.op(v, lambda e: e.memset(small[:, 1022:1023], 0.0), reads=grp, writes=grp)
            for h in range(H):
                hs = hb[hcount % 2]
                hcount += 1
                if h % 4 == 0:
                    P.dma("sync", Vg, VS[n, h // 4], reads=[bVS], writes=[bVg])
                if n == 0:
                    P.dma("sync", hs["q"], QA[h, 0], reads=[bQA], writes=[hs["bq"]])
                    P.dma("sync", hs["qp"], QA[h, 1, 0:64, :], reads=[bQA], writes=[hs["bqp"]])
                    P.dma("sync", hs["k"], QA[h, 2], reads=[bQA], writes=[hs["bk"]])
                    P.dma("sync", hs["g"], PT[CH_GA + h], reads=[bPT], writes=[hs["bg"]])
                else:
                    cq_, ck_, cg_ = ((CH_QB, CH_KB, CH_GB), (CH_QC, CH_KC, CH_GC))[n - 1]
                    P.dma("sync", hs["q"], PT[cq_ + h], reads=[bPT], writes=[hs["bq"]])
                    P.dma("sync", hs["k"], PT[ck_ + h], reads=[bPT], writes=[hs["bk"]])
                    P.dma("sync", hs["g"], PT[cg_ + h], reads=[bPT], writes=[hs["bg"]])
                if n == 1:
                    for i4 in range(4):
                        for ii in range(4):
                            i = i4 * 4 + ii
                            dj = i % 2
                            P.op(v, lambda e, i=i, dj=dj, h=h: e.tensor_scalar(out=dgt[:, dj * 128:(dj + 1) * 128], in0=identf,
                                                                               scalar1=Fcum[:, i * 8 + h:i * 8 + h + 1], scalar2=1.0,
                                                                               op0=ALU.mult, op1=ALU.mult),
                                 reads=[bfc2] + CONST, writes=[bdg[dj]])
                            P.op("tensor", lambda e, ii=ii, dj=dj: e.matmul(ps[6][:, ii * 128:(ii + 1) * 128], lhsT=onesf,
                                                                          rhs=dgt[:, dj * 128:(dj + 1) * 128], start=True, stop=True),
                                 reads=[bdg[dj]] + CONST, writes=[psb[6]])
                        P.op(v, lambda e, i4=i4: e.tensor_copy(out=Frow[:, i4 * 512:(i4 + 1) * 512], in_=ps[6][:, :]),
                             reads=[psb[6]], writes=[bFrow])
                vcol = (h % 4) * 128
                streams = []
                for s_, Is in enumerate(((3, 0), (2, 1))):
                    lst = []
                    for I in Is:
                        Js = list(range(4 * I + 4))
                        if n == 2:
                            Js = Js[::-1]
                        for idx, J in enumerate(Js):
                            lst.append(dict(I=I, J=J, first=(idx == 0), last=(idx == len(Js) - 1),
                                            off=max(0, J - 4 * I) * 128, diag=(J >= 4 * I)))
                    streams.append(lst)
                PSS = ((0, 1), (7, 6) if n != 1 else (7,))

                def G1(s_, k, n=n, h=h, hs=hs, streams=streams, PSS=PSS):
                    st = streams[s_][k]
                    I, J, off, diag = st["I"], st["J"], st["off"], st["diag"]
                    psi = PSS[s_][k % len(PSS[s_])]
                    kcols = slice(J * 128, (J + 1) * 128)
                    qcols = slice(I * 512 + off, (I + 1) * 512)
                    w_ = slice(off, 512)

                    def mqk(e):
                        ins = e.matmul(ps[psi][:, w_], lhsT=hs["k"][:, kcols], rhs=hs["q"][:, qcols], start=True, stop=(n != 0))
                        if n == 0:
                            ins = e.matmul(ps[psi][:, w_], lhsT=kpeT[:, kcols], rhs=hs["qp"][:, qcols], start=False, stop=True)
                        return ins
                    rd = [hs["bq"], hs["bk"]] + ([hs["bqp"], bkpe] if n == 0 else [])
                    P.op("tensor", mqk, reads=rd, writes=[psb[psi]])
                    pt_, bpt_ = pT[s_][k % 2], bpT[s_][k % 2]
                    sp_, bsp_ = spt[s_][k % 2], bspt[s_][k % 2]
                    if n == 0:
                        P.op("scalar", lambda e: e.activation(out=pt_[:, w_], in_=ps[psi][:, w_], func=AF.Exp),
                             reads=[psb[psi]], writes=[bpt_])
                        if diag:
                            P.op("gpsimd", lambda e: e.tensor_tensor(out=pt_[:, off:off + 128], in0=pt_[:, off:off + 128], in1=m_mla, op=ALU.mult),
                                 reads=[bpt_] + CONST, writes=[bpt_])
                    elif n == 1:
                        P.op(v, lambda e: e.tensor_tensor(out=sp_[:, w_], in0=ps[psi][:, w_], in1=Frow[:, qcols], op=ALU.add),
                             reads=[psb[psi], bFrow], writes=[bsp_])
                        if diag:
                            P.op("gpsimd", lambda e: e.tensor_tensor(out=sp_[:, off:off + 128], in0=sp_[:, off:off + 128], in1=mnegf, op=ALU.add),
                                 reads=[bsp_] + CONST, writes=[bsp_])
                        P.op("scalar", lambda e: e.activation(out=pt_[:, w_], in_=sp_[:, w_], func=AF.Exp,
                                                              bias=Cend[:, J * 8 + h:J * 8 + h + 1], scale=1.0),
                             reads=[bsp_, bfc3], writes=[bpt_])
                    else:
                        ln_, bln_ = Lns[s_][k % 2], bLns[s_][k % 2]
                        Rf, bRf = Rfs[s_], bRfs[s_]
                        P.op("scalar", lambda e: e.activation(out=sp_[:, w_], in_=ps[psi][:, w_], func=AF.Exp, scale=-1.0),
                             reads=[psb[psi]], writes=[bsp_])
                        P.op("scalar", lambda e: e.activation(out=sp_[:, w_], in_=sp_[:, w_], func=AF.Ln, bias=1.0, scale=1.0),
                             reads=[bsp_], writes=[bsp_])
                        P.op(v, lambda e: e.tensor_tensor(out=ln_[:, w_], in0=ps[psi][:, w_], in1=sp_[:, w_], op=ALU.add),
                             reads=[psb[psi], bsp_], writes=[bln_])
                        if diag:
                            P.op("gpsimd", lambda e: e.tensor_tensor(out=ln_[:, off:off + 128], in0=ln_[:, off:off + 128], in1=m_sb, op=ALU.mult),
                                 reads=[bln_] + CONST, writes=[bln_])
                        if not st["last"]:
                            rb_, brb_ = Rbv[s_][(k + 1) % 3], bRbv[s_][(k + 1) % 3]
                            if st["first"]:
                                P.op("gpsimd", lambda e: e.memset(Rf[:, :], 0.0), reads=[bRf], writes=[bRf])
                                P.op(v, lambda e: e.tensor_copy(out=Rf[:, w_], in_=ln_[:, w_]), reads=[bln_, bRf], writes=[bRf])
                            else:
                                P.op("gpsimd", lambda e: e.tensor_tensor(out=Rf[:, w_], in0=Rf[:, w_], in1=ln_[:, w_], op=ALU.add),
                                     reads=[bln_, bRf], writes=[bRf])
                            P.op("scalar", lambda e: e.copy(out=rb_[:, :], in_=Rf[:, :]), reads=[bRf], writes=[brb_])

                def G2(s_, k, n=n, h=h, hs=hs, streams=streams):
                    st = streams[s_][k]
                    off, diag = st["off"], st["diag"]
                    w_ = slice(off, 512)
                    pa = 4 + s_
                    sp_, bsp_ = spt[s_][k % 2], bspt[s_][k % 2]
                    ln_, bln_ = Lns[s_][k % 2], bLns[s_][k % 2]
                    at_, bat_ = pT[s_][k % 2], bpT[s_][k % 2]
                    rb_, brb_ = Rbv[s_][k % 3], bRbv[s_][k % 3]
                    first = st["first"]

                    def mcs(e):
                        ins = e.matmul(ps[pa][:, w_], lhsT=negU, rhs=ln_[:, w_], start=True, stop=first)
                        if not first:
                            ins = e.matmul(ps[pa][:, w_], lhsT=negones, rhs=rb_[:, w_], start=False, stop=True)
                        return ins
                    P.op("tensor", mcs, reads=[bln_] + ([] if first else [brb_]) + CONST, writes=[psb[pa]])
                    P.op(v, lambda e: e.tensor_tensor(out=sp_[:, w_], in0=ps[pa][:, w_], in1=sp_[:, w_], op=ALU.subtract),
                         reads=[psb[pa], bsp_], writes=[bsp_])
                    P.op("scalar", lambda e: e.activation(out=at_[:, w_], in_=sp_[:, w_], func=AF.Exp), reads=[bsp_], writes=[bat_])
                    if diag:
                        P.op("gpsimd", lambda e: e.tensor_tensor(out=at_[:, off:off + 128], in0=at_[:, off:off + 128], in1=m_sb, op=ALU.mult),
                             reads=[bat_] + CONST, writes=[bat_])

                def G3(s_, k, n=n, h=h, hs=hs, streams=streams, vcol=vcol):
                    st = streams[s_][k]
                    I, J, off = st["I"], st["J"], st["off"]
                    po, pr = 2 + s_, 4 + s_
                    w_ = slice(off, 512)
                    pt_, bpt_ = pT[s_][k % 2], bpT[s_][k % 2]
                    vv = Vg[:, J * 512 + vcol: J * 512 + vcol + 128]
                    rec, brec = recs[s_], brecs[s_]
                    if st["first"]:
                        P.op("tensor", lambda e: e.matmul(ps[po][:, :], lhsT=zeros_b, rhs=cst[:, 0:512], start=True, stop=False),
                             reads=CONST, writes=[psb[po]])
                        if n != 2:
                            P.op("tensor", lambda e: e.matmul(ps[pr][:, :], lhsT=zeros_b, rhs=cst[:, 0:512], start=True, stop=False),
                                 reads=CONST, writes=[psb[pr]])
                    lastj = st["last"]
                    if n != 2:
                        def mpv(e):
                            ins = e.matmul(ps[po][:, w_], lhsT=vv, rhs=pt_[:, w_], start=False, stop=lastj)
                            ins = e.matmul(ps[pr][:, w_], lhsT=ones_b, rhs=pt_[:, w_], start=False, stop=lastj)
                            return ins
                        P.op("tensor", mpv, reads=[bpt_, bVg] + CONST, writes=[psb[po], psb[pr]])
                    else:
                        P.op("tensor", lambda e: e.matmul(ps[po][:, w_], lhsT=vv, rhs=pt_[:, w_], start=False, stop=lastj),
                             reads=[bpt_, bVg], writes=[psb[po]])
                    if lastj:
                        yj = ycount[0] % 2
                        ycount[0] += 1
                        ys_, bys_ = yst[yj], byst[yj]
                        gsl = hs["g"][:, I * 512:(I + 1) * 512]
                        if n != 2:
                            P.op(v, lambda e: e.reciprocal(out=rec, in_=ps[pr][:, :]), reads=[psb[pr]], writes=[brec])
                            P.op("gpsimd", lambda e: e.tensor_tensor(out=rec, in0=rec, in1=gsl, op=ALU.mult), reads=[brec, hs["bg"]], writes=[brec])
                            P.op(v, lambda e: e.tensor_tensor(out=ys_, in0=ps[po][:, :], in1=rec, op=ALU.mult),
                                 reads=[psb[po], brec], writes=[bys_])
                        else:
                            P.op(v, lambda e: e.tensor_tensor(out=ys_, in0=ps[po][:, :], in1=gsl, op=ALU.mult),
                                 reads=[psb[po], hs["bg"]], writes=[bys_])
                        P.dma("scalar", YS[:, n * H + h, I * 512:(I + 1) * 512], ys_, reads=[bys_], writes=[bYS])

                NSm = max(len(streams[0]), len(streams[1]))
                for t in range(NSm + 2):
                    for s_ in range(2):
                        if t < len(streams[s_]):
                            G1(s_, t)
                    if n == 2:
                        for s_ in range(2):
                            if 0 <= t - 1 < len(streams[s_]):
                                G2(s_, t - 1)
                        for s_ in range(2):
                            if 0 <= t - 2 < len(streams[s_]):
                                G3(s_, t - 2)
                    else:
                        for s_ in range(2):
                            if 0 <= t - 1 < len(streams[s_]):
                                G3(s_, t - 1)
        P.op(v, lambda e: e.memset(small[:, 1019:1020], 0.0), reads=grp0, writes=grp0)
        do_fence()

        if maxphase <= 3:
            break
        bmg = [buf(f"uT{r}") for r in range(NR)]
        ysr = arena[:, 0:12288]
        bysr = pbuf("ysr")
        sgt = [[arena[:, 12288 + (j * 3 + n) * 512:12288 + (j * 3 + n + 1) * 512] for n in range(3)] for j in range(2)]
        bsg = [pbuf(f"sg{j}") for j in range(2)]
        acc = [a32[:, 7680 + j * 512:7680 + (j + 1) * 512] for j in range(2)]
        bacc = [pbuf(f"acc{j}") for j in range(2)]
        tmp = [a32[:, 8704 + j * 512:8704 + (j + 1) * 512] for j in range(2)]
        btmp = [pbuf(f"tmp{j}") for j in range(2)]
        for r in range(NR):
            P.dma("sync", ysr.rearrange("p (h t) -> p h t", t=512), YS[:, :, r * 512:(r + 1) * 512], reads=[bYS], writes=[bysr])
            for dc in range(KC):
                j = dc % 2
                wbs = [load_cast(w_br[l, n, dc], 1024, cast_eng=("vector" if n != 1 else "gpsimd")) for n in range(3)]
                for n in range(3):
                    P.dma("sync", sgt[j][n], PT[CH_M + n * KC + dc, :, r * 512:(r + 1) * 512], reads=[bPT], writes=[bsg[j]])
                pbase = 3 * j
                for n in range(3):
                    wt, wtb = wbs[n]

                    def mm(e, n=n, wt=wt, pbase=pbase):
                        ins = None
                        for wc in range(8):
                            ins = e.matmul(ps[pbase + n][:, :], lhsT=wt[:, wc * 128:(wc + 1) * 128],
                                           rhs=ysr[:, (n * 8 + wc) * 512:(n * 8 + wc + 1) * 512], start=(wc == 0), stop=(wc == 7))
                        return ins
                    P.op("tensor", mm, reads=[wtb, bysr], writes=[psb[pbase + n]])
                P.op(v, lambda e, j=j, pbase=pbase: e.tensor_tensor(out=acc[j], in0=ps[pbase][:, :], in1=sgt[j][0], op=ALU.mult),
                     reads=[psb[pbase], bsg[j]], writes=[bacc[j]])
                P.op("gpsimd" if False else v, lambda e, j=j, pbase=pbase: e.tensor_tensor(out=tmp[j], in0=ps[pbase + 1][:, :], in1=sgt[j][1], op=ALU.mult),
                     reads=[psb[pbase + 1], bsg[j]], writes=[btmp[j]])
                P.op("gpsimd", lambda e, j=j: e.tensor_tensor(out=acc[j], in0=acc[j], in1=tmp[j], op=ALU.add),
                     reads=[bacc[j], btmp[j]], writes=[bacc[j]])
                P.op(v, lambda e, j=j, pbase=pbase: e.tensor_tensor(out=tmp[j], in0=ps[pbase + 2][:, :], in1=sgt[j][2], op=ALU.mult),
                     reads=[psb[pbase + 2], bsg[j]], writes=[btmp[j]])
                P.op("gpsimd", lambda e, j=j, dc=dc, r=r: e.tensor_tensor(out=uT[:, dc * S + r * 512: dc * S + (r + 1) * 512], in0=acc[j], in1=tmp[j], op=ALU.add),
                     reads=[bacc[j], btmp[j]], writes=[bmg[r]])
        do_fence()

        if maxphase <= 4:
            break
        wo = [arena[:, j * 8192:(j + 1) * 8192] for j in range(2)]
        bwo = [pbuf(f"wo{j}") for j in range(2)]
        gpc = a32[:, 8192:8704]
        bgp = pbuf("gpc")
        xp = [a32[:, 8704 + j * 512:8704 + (j + 1) * 512] for j in range(2)]
        bxp = [pbuf(f"xp{j}") for j in range(2)]
        yp = [a32[:, 9728 + j * 512:9728 + (j + 1) * 512] for j in range(2)]
        byp = [pbuf(f"yp{j}") for j in range(2)]
        bZS = buf("ZS")
        wo_v = w_out[l].rearrange("(dc p) e -> p dc e", p=128)
        for er in range(NR):
            j = er % 2
            P.dma("sync", gpc, GR[:, er * 512:(er + 1) * 512], reads=[bGR], writes=[bgp])
            for piece in range(4):
                i = cnt["wst"] % 2
                cnt["wst"] += 1
                P.dma("sync", wst[i][:, :].rearrange("p (a b) -> p a b", b=512), wo_v[:, piece * 4:(piece + 1) * 4, er * 512:(er + 1) * 512],
                      writes=[wst_b[i]])
                for q4 in range(4):
                    P.op(v if q4 % 2 else "gpsimd",
                         lambda e, i=i, piece=piece, q4=q4, j=j: e.tensor_tensor(out=wo[j][:, (piece * 4 + q4) * 512:(piece * 4 + q4 + 1) * 512],
                                                                               in0=wst[i][:, q4 * 512:(q4 + 1) * 512], in1=gpc, op=ALU.mult),
                         reads=[wst_b[i], bgp], writes=[bwo[j]])
            for tb in range(NB):
                pi_ = nps(6)
                k = tb % 2

                def mm(e, tb=tb, pi_=pi_, j=j):
                    ins = None
                    for dc in range(KC):
                        ins = e.matmul(ps[pi_][:, :], lhsT=uT[:, dc * S + tb * 128: dc * S + (tb + 1) * 128],
                                       rhs=wo[j][:, dc * 512:(dc + 1) * 512], start=(dc == 0), stop=(dc == KC - 1))
                    return ins
                P.op("tensor", mm, reads=[bwo[j]] + list(bmg), writes=[psb[pi_]])
                P.dma("sync", xp[k], x_src[tb * 128:(tb + 1) * 128, er * 512:(er + 1) * 512], reads=[x_srcb], writes=[bxp[k]])
                P.op(v, lambda e, k=k, pi_=pi_: e.scalar_tensor_tensor(out=yp[k], in0=xp[k], scalar=float(ALPHA), in1=ps[pi_][:, :],
                                                                      op0=ALU.mult, op1=ALU.add),
                     reads=[bxp[k], psb[pi_]], writes=[byp[k]])
                P.dma("scalar", ZS[tb * 128:(tb + 1) * 128, er * 512:(er + 1) * 512], yp[k], reads=[byp[k]], writes=[bZS])
        do_fence()

        if maxphase <= 5:
            break
        lg = a32[:, 0:2048]
        lb = a32[:, 2048:4096]
        blg = pbuf("lnrows")
        P.dma("sync", lg, ln_g[l].partition_broadcast(128), writes=[blg])
        P.dma("sync", lb, ln_b[l].partition_broadcast(128), writes=[blg])
        for tb in range(NB):
            sl = tb % 2
            zb = a32[:, 4096 + sl * 2048:4096 + (sl + 1) * 2048]
            ob = a32[:, 8192 + sl * 2048:8192 + (sl + 1) * 2048]
            bz, bo, bst = pbuf(f"p5z{sl}"), pbuf(f"p5o{sl}"), pbuf(f"p5st{sl}")
            P.dma(dq(), zb, ZS[tb * 128:(tb + 1) * 128, :], reads=[bZS], writes=[bz])
            st2 = small[:, 912 + sl * 24:912 + (sl + 1) * 24]
            mv = mvs[:, 8 + sl * 4:8 + sl * 4 + 2]
            rs = mvs[:, 8 + sl * 4 + 2:8 + sl * 4 + 3]
            for q4 in range(4):
                P.op(v, lambda e, q4=q4, zb=zb, st2=st2: e.bn_stats(out=st2[:, q4 * 6:(q4 + 1) * 6], in_=zb[:, q4 * 512:(q4 + 1) * 512]),
                     reads=[bz], writes=[bst])
            P.op(v, lambda e, mv=mv, st2=st2: e.bn_aggr(out=mv, in_=st2), reads=[bst], writes=[bst])
            P.op(v, lambda e, mv=mv, rs=rs: e.tensor_scalar(out=rs, in0=mv[:, 1:2], scalar1=LN_EPS, scalar2=1.0, op0=ALU.add, op1=ALU.mult),
                 reads=[bst], writes=[bst])
            P.op("scalar", lambda e, rs=rs: e.activation(out=rs, in_=rs, func=AF.Sqrt), reads=[bst], writes=[bst])
            P.op(v, lambda e, rs=rs: e.reciprocal(out=rs, in_=rs), reads=[bst], writes=[bst])
            P.op(v, lambda e, zb=zb, ob=ob, mv=mv, rs=rs: e.tensor_scalar(out=ob, in0=zb, scalar1=mv[:, 0:1], scalar2=rs,
                                                                         op0=ALU.subtract, op1=ALU.mult),
                 reads=[bz, bst], writes=[bo])
            P.op("gpsimd", lambda e, ob=ob: e.tensor_tensor(out=ob, in0=ob, in1=lg, op=ALU.mult), reads=[bo, blg], writes=[bo])
            P.op(v, lambda e, ob=ob: e.tensor_tensor(out=ob, in0=ob, in1=lb, op=ALU.add), reads=[bo, blg], writes=[bo])
            o_ = P.dma("scalar", x_dst[tb * 128:(tb + 1) * 128, :], ob, reads=[bo], writes=[x_dstb])
            if last:
                final_ops.append(o_)
        do_fence()
        x_src, x_srcb = x_dst, x_dstb

    if maxphase < 99:
        final_ops.append(P.dma('sync', y_out[0:128, 0:2048], uT[:, 0:4096].bitcast(F32), reads=[buf('uT0'), buf('uT1'), buf('uT2'), buf('uT3')], writes=[buf('y_out')]))
    P.emit(final_ops)
    for cm in reversed(ctx):
        cm.__exit__(None, None, None)
    return nc, P


def _prep_layer_inputs(inp, layers):
    f = np.float32
    w_in = inp["w_in"]
    L = len(layers)
    w_ch = np.empty((L, NCH, 128, KC * 128), f)
    w_wide = np.empty((L, 4, 128, KC * 512), f)
    w_f = np.empty((L, 128, KC * 8), f)

    def chunk(W, cols):
        n = len(cols)
        return np.ascontiguousarray(W[:, cols].reshape(KC, 128, n).transpose(1, 0, 2)).reshape(128, KC * n)

    ar = np.arange
    for li, l in enumerate(layers):
        W = w_in[l]
        cols = []
        for i in range(4):
            cols.append(ar(i * 128, (i + 1) * 128))
        for i in range(2):
            cols.append(ar(512 + i * 128, 512 + (i + 1) * 128))
        cols.append(np.concatenate([ar(768, 832), ar(800, 832), ar(768, 800)]))
        for h in range(H):
            cols.append(ar(832 + h * 128, 832 + (h + 1) * 128))
        for base in (1856, 2880, 4936, 5960, 6984, 9032):
            for h in range(H):
                cols.append(ar(base + h * 128, base + (h + 1) * 128))
        for i in range(48):
            cols.append(ar(10056 + i * 128, 10056 + (i + 1) * 128))
        assert len(cols) == NCH
        for c, cc in enumerate(cols):
            w_ch[li, c] = chunk(W, cc)
        for wi, base in enumerate((3904, 3904 + 512, 8008, 8008 + 512)):
            w_wide[li, wi] = chunk(W, ar(base, base + 512))
        w_f[li] = chunk(W, ar(4928, 4936))
    d = {}
    d["w_ch"], d["w_wide"], d["w_f"] = w_ch, w_wide, w_f
    ls = list(layers)
    d["w_ada"] = np.ascontiguousarray(inp["w_ada"][ls])
    b_ada = inp["b_ada"][ls]
    d["b_adaT"] = np.ascontiguousarray(b_ada.reshape(L, 48, 128).transpose(0, 2, 1))
    d["gate_b"] = np.ascontiguousarray(b_ada[:, None, 2 * D:3 * D])
    d["gq"] = np.ascontiguousarray(inp["q_norm_g"][ls].reshape(L, 4, 128).transpose(0, 2, 1))
    d["gkv"] = np.ascontiguousarray(inp["kv_norm_g"][ls].reshape(L, 2, 128).transpose(0, 2, 1))
    wuq = inp["w_uq"][ls].reshape(L, 4, 128, H, 192)
    nope = wuq[..., 0:128]
    ropec = wuq[..., 128:192]
    rot = np.concatenate([wuq[..., 160:192], wuq[..., 128:160]], axis=-1)
    allq = np.concatenate([nope, ropec, rot], axis=-1)
    d["w_uq"] = np.ascontiguousarray(allq.transpose(0, 3, 2, 1, 4)).reshape(L, H, 128, 4 * 256)
    wukv = inp["w_ukv"][ls].reshape(L, 2, 128, H, 256)
    d["w_uk"] = np.ascontiguousarray(wukv[..., 0:128].transpose(0, 3, 2, 1, 4)).reshape(L, H, 128, 2 * 128)
    vv = wukv[..., 128:256].reshape(L, 2, 128, 2, 4 * 128)
    d["w_uv"] = np.ascontiguousarray(vv.transpose(0, 3, 2, 1, 4)).reshape(L, 2, 128, 2 * 512)
    d["fbias"] = np.ascontiguousarray(inp["fox_bias"][ls][:, None, :])
    wb = inp["w_branch"][ls].reshape(L, 3, 8, 128, KC, 128)
    d["w_br"] = np.ascontiguousarray(wb.transpose(0, 1, 4, 3, 2, 5)).reshape(L, 3, KC, 128, 8 * 128)
    d["w_out"] = np.ascontiguousarray(inp["w_out"][ls])
    d["ln_g"] = np.ascontiguousarray(inp["ln_g"][ls][:, None, :])
    d["ln_b"] = np.ascontiguousarray(inp["ln_b"][ls][:, None, :])
    return d


_CACHE = {}


MAXPHASE = 99


def _get_prog(L):
    if L not in _CACHE:
        _CACHE[L] = build_program(L, maxphase=MAXPHASE)[0]
    return _CACHE[L]


def _run(inp, x, layers):
    nb = x.shape[0]
    shared = _prep_layer_inputs(inp, layers)
    half = 32
    invf = (10000.0 ** (-np.arange(half, dtype=np.float32) / np.float32(half))).astype(np.float32)
    invf64 = np.concatenate([invf, invf])[:, None].astype(np.float32)
    sgn = np.concatenate([-np.ones(32, np.float32), np.ones(32, np.float32)])[:, None]
    in_maps = []
    for b in range(nb):
        m = dict(shared)
        m["x"] = np.ascontiguousarray(x[b])
        m["cT"] = np.ascontiguousarray(inp["c"][b].reshape(KC, 128).T)
        m["pos"] = np.ascontiguousarray(inp["positions"][b][None, :].astype(np.int32))
        m["invf"] = invf64
        m["sgn"] = sgn
        in_maps.append(m)
    nc = _get_prog(len(layers))
    res = run_bass_kernel_spmd(nc, in_maps, core_ids=list(range(nb)))
    return np.stack([r["y"] for r in res.results], axis=0)


N_FUSED_LAYERS = 4


def kernel(**inputs):
    inp = {k: np.asarray(v) for k, v in inputs.items()}
    x = inp["x"].astype(np.float32)
    l0 = 0
    while l0 < DEPTH:
        ls = list(range(l0, min(DEPTH, l0 + N_FUSED_LAYERS)))
        x = _run(inp, x, ls)
        l0 += len(ls)
    return x.astype(np.float32)
```

```python
import math
import numpy as np
import concourse.bass as bass
import concourse.mybir as mybir
from concourse.bass_utils import run_bass_kernel_spmd

F32 = mybir.dt.float32
BF16 = mybir.dt.bfloat16
I32 = mybir.dt.int32
AF = mybir.ActivationFunctionType
ALU = mybir.AluOpType

D = 2048
S = 2048
DEPTH = 4
NB = 16
NR = 4
KC = 16
H = 8
ALPHA = (2 * DEPTH) ** 0.25
LN_EPS = 1e-5
RMS_EPS = 1e-6
NCORES = 4
SEM_CAP = 32000

CH_CQ, CH_CKV, CH_KR, CH_GA = 0, 4, 6, 7
CH_QB, CH_KB, CH_GB = 15, 23, 31
CH_QC, CH_KC, CH_GC = 39, 47, 55
CH_M = 63
NCH = 111


class Buf:
    __slots__ = ("name", "writers", "readers", "prev")

    def __init__(self, name):
        self.name = name
        self.writers = []
        self.readers = []
        self.prev = []


class Op:
    __slots__ = ("eng", "fn", "deps", "is_dma", "sem", "val", "signal", "idx", "wname")


class Prog:
    def __init__(self, nc):
        self.nc = nc
        self.ops = []

    def _add(self, eng, fn, reads, writes, is_dma):
        op = Op()
        op.eng, op.fn, op.is_dma = eng, fn, is_dma
        op.sem, op.val, op.signal = None, 0, False
        op.idx = len(self.ops)
        op.wname = writes[0].name if writes else "none"
        deps = {}
        for b in reads:
            for w in b.writers:
                deps[w.idx] = w
            if not b.writers:
                for a in b.prev:
                    deps[a.idx] = a
        for b in writes:
            if b.readers:
                b.prev = b.readers + b.writers
                b.readers = []
                b.writers = []
            for a in b.prev:
                deps[a.idx] = a
            if b.writers:
                deps[b.writers[-1].idx] = b.writers[-1]
        for b in reads:
            b.readers.append(op)
        for b in writes:
            b.writers.append(op)
        op.deps = list(deps.values())
        self.ops.append(op)
        return op

    def op(self, eng, fn, reads=(), writes=()):
        return self._add(eng, fn, list(reads), list(writes), False)

    def dma(self, eng, out, in_, reads=(), writes=()):
        return self._add(eng, lambda e: e.dma_start(out=out, in_=in_), list(reads), list(writes), True)

    def emit(self, final_ops):
        nc = self.nc

        def skip(d, op):
            return (not d.is_dma) and (not op.is_dma) and d.eng == op.eng == "tensor"

        for op in self.ops:
            for d in op.deps:
                if d.is_dma or skip(d, op):
                    continue
                d.signal = True
        CAP = SEM_CAP
        sem_ctx = []

        def new_sem(name):
            cm = nc.semaphore(name)
            s = cm.__enter__()
            sem_ctx.append(cm)
            return s

        eng_sem, eng_cnt = {}, {}
        dsem = {}
        nsem = 0
        for op in self.ops:
            if op.is_dma:
                ent = dsem.get(op.wname)
                if ent is None or ent[1] + 16 > CAP:
                    ent = [new_sem(f"d{nsem}"), 0]
                    nsem += 1
                    dsem[op.wname] = ent
                ent[1] += 16
                op.sem, op.val = ent[0], ent[1]
            elif op.signal:
                e = op.eng
                if e not in eng_sem or eng_cnt[e] >= CAP:
                    eng_sem[e] = new_sem(f"e{nsem}")
                    nsem += 1
                    eng_cnt[e] = 0
                eng_cnt[e] += 1
                op.sem, op.val = eng_sem[e], eng_cnt[e]
        self.nsem = nsem
        per_eng = {}
        for op in self.ops:
            per_eng.setdefault(op.eng, []).append(op)
        final = list(final_ops)
        with nc.Block() as block:
            def make(engname, oplist):
                def body(e):
                    waited = {}
                    for op in oplist:
                        need = {}
                        for d in op.deps:
                            if skip(d, op):
                                continue
                            k = id(d.sem)
                            if k not in need or need[k][1] < d.val:
                                need[k] = (d.sem, d.val)
                        for k, (s_, v_) in need.items():
                            if waited.get(k, 0) >= v_:
                                continue
                            e.wait_ge(s_, v_)
                            waited[k] = v_
                        ins = op.fn(e)
                        if op.sem is not None:
                            ins.then_inc(op.sem, 16 if op.is_dma else 1)
                    if engname == "sync":
                        for f in final:
                            e.wait_ge(f.sem, f.val)
                return body
            for engname in ("sync", "tensor", "vector", "scalar", "gpsimd"):
                getattr(block, engname)(make(engname, per_eng.get(engname, [])))
        for cm in reversed(sem_ctx):
            cm.__exit__(None, None, None)


def build_program(L, dbg=False, maxphase=99):
    nc = bass.Bass("TRN2", target_bir_lowering=False)

    def din(name, shape, dt=F32):
        return nc.dram_tensor(name, list(shape), dt, kind="ExternalInput").ap()

    def dscr(name, shape, dt):
        return nc.dram_tensor(name, list(shape), dt, kind="Internal").ap()

    x_in = din("x", [S, D])
    cT_in = din("cT", [128, KC])
    pos_in = din("pos", [1, S], I32)
    invf_in = din("invf", [64, 1])
    sgn_in = din("sgn", [64, 1])
    w_ada = din("w_ada", [L, D, 3 * D])
    b_adaT = din("b_adaT", [L, 128, 48])
    gate_b = din("gate_b", [L, 1, D])
    w_ch = din("w_ch", [L, NCH, 128, KC * 128])
    w_wide = din("w_wide", [L, 4, 128, KC * 512])
    w_f = din("w_f", [L, 128, KC * 8])
    gq_in = din("gq", [L, 128, 4])
    gkv_in = din("gkv", [L, 128, 2])
    w_uq = din("w_uq", [L, H, 128, 4 * 256])
    w_uk = din("w_uk", [L, H, 128, 2 * 128])
    w_uv = din("w_uv", [L, 2, 128, 2 * 512])
    fbias = din("fbias", [L, 1, 8])
    w_br = din("w_br", [L, 3, KC, 128, 8 * 128])
    w_out = din("w_out", [L, D, D])
    ln_g = din("ln_g", [L, 1, D])
    ln_b = din("ln_b", [L, 1, D])
    y_out = nc.dram_tensor("y", [S, D], F32, kind="ExternalOutput").ap()

    xs = [dscr("xs0", [S, D], F32), dscr("xs1", [S, D], F32)]
    PT = dscr("PT", [NCH, 128, S], BF16)
    QA = dscr("QA", [H, 3, 128, S], BF16)
    VS = dscr("VS", [3, 2, 128, NB * 512], BF16)
    YS = dscr("YS", [128, 3 * H, S], BF16)
    GR = dscr("GR", [128, D], F32)
    ZS = dscr("ZS", [S, D], F32)

    P = Prog(nc)
    ctx = []

    def sb(name, shape, dt):
        cm = nc.sbuf_tensor(name, list(shape), dt)
        t = cm.__enter__()
        ctx.append(cm)
        return t

    def psum(name, shape, dt):
        cm = nc.psum_tensor(name, list(shape), dt)
        t = cm.__enter__()
        ctx.append(cm)
        return t

    uT = sb("uT", [128, KC * S], BF16)
    arena = sb("arena", [128, 32768], BF16)
    wst = [sb(f"wst{i}", [128, 2048], F32) for i in range(2)]
    wbf = [sb(f"wbf{i}", [128, 2048], BF16) for i in range(5)]
    cos2 = sb("cos2", [64, S], BF16)
    sin2 = sb("sin2", [64, S], BF16)
    kpeT = sb("kpeT", [64, S], BF16)
    small = sb("small", [128, 1024], F32)
    cst = sb("cst", [128, 1152], BF16)
    cstf = sb("cstf", [128, 768], F32)
    dgt = sb("dgt", [128, 256], F32)
    a32 = arena[:, :].bitcast(F32)
    ai32 = arena[:, :].bitcast(I32)

    B = {}

    def buf(name):
        if name not in B:
            B[name] = Buf(name)
        return B[name]

    ps = [psum(f"ps{i}", [128, 512], F32) for i in range(8)]
    psb = [buf(f"ps{i}") for i in range(8)]

    ident = cst[:, 0:128]
    m_fox = cst[:, 128:256]
    m_sb = cst[:, 256:384]
    m_mla = cst[:, 384:512]
    negU = cst[:, 512:640]
    negones = cst[:, 640:768]
    ones_b = cst[:, 768:896]
    cT_b = cst[:, 896:912]
    zeros_b = cst[:, 1024:1152]
    onesf = cstf[:, 0:128]
    negtri = cstf[:, 128:256]
    sel127 = cstf[:, 256:384]
    negonesf = cstf[:, 384:512]
    identf = cstf[:, 512:640]
    mnegf = cstf[:, 640:768]
    bconst = buf("const")

    modT = small[:, 0:32]
    sc1T = small[:, 48:64]
    stats = small[:, 64:88]
    cT_f = small[:, 96:112]
    gq_s = small[:, 112:116]
    gkv_s = small[:, 116:118]
    fb_s = small[:, 120:128]
    badaT = small[:, 128:176]
    invf = small[0:64, 176:177]
    sgn = small[0:64, 177:178]
    gateT = small[:, 178:194]
    Fcum = small[:, 192:320]
    Cend = small[:, 320:448]
    SPf = small[:, 448:576]
    fbt = small[:, 576:832]
    mvs = small[:, 832:896]
    rkv = small[:, 896:912]

    phase_bufs = []

    def abuf(name):
        b = buf(name)
        if b not in phase_bufs:
            phase_bufs.append(b)
        return b

    def fence():
        nonlocal phase_bufs
        if not phase_bufs:
            return
        old = phase_bufs
        f = P.op("vector", lambda e: e.memset(small[:, 1020:1021], 0.0), reads=old, writes=old)
        for b in old:
            b.prev = [f]
            b.writers = []
            b.readers = []
        for name in list(B.keys()):
            if B[name] in old:
                del B[name]
        phase_bufs = []
        return f

    fence_op = [None]

    def pbuf(name):
        if name not in B:
            b = abuf(name)
            if fence_op[0] is not None:
                b.prev = [fence_op[0]]
        return B[name]

    def do_fence():
        f = fence()
        if f is not None:
            fence_op[0] = f

    cnt = {"wst": 0, "wbf": 0, "q": 0, "ps": 0, "ev": 0}
    wst_b = [buf(f"wst{i}") for i in range(2)]
    wbf_b = [buf(f"wbf{i}") for i in range(5)]

    def dq():
        cnt["q"] += 1
        return "sync"

    def load_cast(src_ap, n_el, scale_ap=None, nk=1, cast_eng="vector", extra_reads=()):
        i = cnt["wst"] % 2
        cnt["wst"] += 1
        j = cnt["wbf"] % 5
        cnt["wbf"] += 1
        st, stb = wst[i], wst_b[i]
        wt, wtb = wbf[j], wbf_b[j]
        P.dma("sync", st[:, 0:n_el], src_ap, writes=[stb])
        if scale_ap is None:
            P.op(cast_eng, lambda e: e.tensor_copy(out=wt[:, 0:n_el], in_=st[:, 0:n_el]), reads=[stb], writes=[wtb])
        else:
            w = n_el // nk
            for kk in range(nk):
                P.op(cast_eng, lambda e, kk=kk: e.tensor_scalar(out=wt[:, kk * w:(kk + 1) * w], in0=st[:, kk * w:(kk + 1) * w],
                                                               scalar1=scale_ap[:, kk:kk + 1], scalar2=1.0,
                                                               op0=ALU.mult, op1=ALU.mult),
                     reads=[stb] + list(extra_reads), writes=[wtb])
        return wt, wtb

    def nps(n=6):
        i = cnt["ps"] % n
        cnt["ps"] += 1
        return i

    def ev_eng():
        cnt["ev"] += 1
        return "vector" if cnt["ev"] % 2 else "scalar"

    def copy_scaled(eng, out, in_, scale=None):
        if eng == "scalar":
            if scale is None:
                return lambda e: e.copy(out=out, in_=in_)
            return lambda e: e.mul(out=out, in_=in_, mul=float(scale))
        if scale is None:
            return lambda e: e.tensor_copy(out=out, in_=in_)
        return lambda e: e.tensor_scalar(out=out, in0=in_, scalar1=float(scale), scalar2=1.0, op0=ALU.mult, op1=ALU.mult)

    g = "gpsimd"
    P.op(g, lambda e: e.memset(cst[:, :], 0.0), writes=[bconst])
    P.op(g, lambda e: e.memset(cstf[:, :], 0.0), writes=[bconst])
    P.op(g, lambda e: e.memset(ones_b, 1.0), writes=[bconst])
    P.op(g, lambda e: e.memset(negones, -1.0), writes=[bconst])
    P.op(g, lambda e: e.memset(onesf, 1.0), writes=[bconst])
    P.op(g, lambda e: e.memset(negonesf, -1.0), writes=[bconst])
    bc2 = buf("const2")
    P.op(g, lambda e: e.affine_select(out=ident, in_=ident, pattern=[[-1, 128]], compare_op=ALU.not_equal,
                                      fill=1.0, base=0, channel_multiplier=1), reads=[bconst], writes=[bc2])
    P.op(g, lambda e: e.affine_select(out=m_fox, in_=ones_b, pattern=[[1, 128]], compare_op=ALU.is_ge,
                                      fill=0.0, base=0, channel_multiplier=-1), reads=[bconst], writes=[bc2])
    P.op(g, lambda e: e.affine_select(out=m_sb, in_=ones_b, pattern=[[1, 128]], compare_op=ALU.is_gt,
                                      fill=0.0, base=0, channel_multiplier=-1), reads=[bconst], writes=[bc2])
    P.op(g, lambda e: e.memset(m_mla, 1.0), reads=[bconst], writes=[bc2])
    bc3 = buf("const3")
    P.op(g, lambda e: e.memset(cst[64:128, 384:448], 0.0), reads=[bc2], writes=[bc3])
    P.op(g, lambda e: e.affine_select(out=negU, in_=negones, pattern=[[-1, 128]], compare_op=ALU.is_gt,
                                      fill=0.0, base=0, channel_multiplier=1), reads=[bconst], writes=[bc2])
    P.op(g, lambda e: e.affine_select(out=negtri, in_=negonesf, pattern=[[1, 128]], compare_op=ALU.is_ge,
                                      fill=0.0, base=0, channel_multiplier=-1), reads=[bconst], writes=[bc2])
    P.op(g, lambda e: e.affine_select(out=sel127, in_=onesf, pattern=[[0, 128]], compare_op=ALU.is_equal,
                                      fill=0.0, base=-127, channel_multiplier=1), reads=[bconst], writes=[bc2])
    P.op(g, lambda e: e.affine_select(out=identf, in_=identf, pattern=[[-1, 128]], compare_op=ALU.not_equal,
                                      fill=1.0, base=0, channel_multiplier=1), reads=[bconst], writes=[bc2])
    P.op(g, lambda e: e.affine_select(out=mnegf, in_=mnegf, pattern=[[1, 128]], compare_op=ALU.is_ge,
                                      fill=-30000.0, base=0, channel_multiplier=-1), reads=[bconst], writes=[bc2])
    CONST = [bconst, bc2, bc3]
    bsm = buf("small_c")
    P.dma("sync", cT_f, cT_in[:, :], writes=[bsm])
    P.dma("sync", invf, invf_in[:, :], writes=[bsm])
    P.dma("sync", sgn, sgn_in[:, :], writes=[bsm])
    bc4 = buf("const4")
    P.op("vector", lambda e: e.tensor_copy(out=cT_b, in_=cT_f), reads=[bsm, bconst], writes=[bc4])
    CONST.append(bc4)
    posi = ai32[0:64, 0:S]
    r1 = a32[0:64, S:2 * S]
    ri = ai32[0:64, 2 * S:3 * S]
    rf = a32[0:64, 3 * S:4 * S]
    fr = a32[0:64, 4 * S:5 * S]
    bt = [pbuf(f"rope_t{i}") for i in range(5)]
    P.dma("sync", posi, pos_in.partition_broadcast(64), writes=[bt[0]])
    v = "vector"
    P.op(v, lambda e: e.tensor_copy(out=r1, in_=posi), reads=[bt[0]], writes=[bt[1]])
    P.op(v, lambda e: e.tensor_scalar(out=r1, in0=r1, scalar1=invf, scalar2=float(1.0 / (2 * math.pi)),
                                      op0=ALU.mult, op1=ALU.mult), reads=[bt[1], bsm], writes=[bt[1]])
    bc5 = buf("const5")
    for which, dst in ((0, sin2), (1, cos2)):
        if which == 1:
            P.op(v, lambda e: e.tensor_scalar(out=r1, in0=r1, scalar1=0.25, scalar2=1.0, op0=ALU.add, op1=ALU.mult),
                 reads=[bt[1]], writes=[bt[1]])
        P.op(v, lambda e: e.tensor_copy(out=ri, in_=r1), reads=[bt[1]], writes=[bt[2]])
        P.op(v, lambda e: e.tensor_copy(out=rf, in_=ri), reads=[bt[2]], writes=[bt[3]])
        P.op(v, lambda e: e.tensor_tensor(out=fr, in0=r1, in1=rf, op=ALU.subtract), reads=[bt[1], bt[3]], writes=[bt[4]])
        P.op("scalar", lambda e: e.activation(out=fr, in_=fr, func=AF.Sin, scale=float(2 * math.pi)),
             reads=[bt[4]], writes=[bt[4]])
        if which == 0:
            P.op(v, lambda e, dst=dst: e.tensor_scalar(out=dst[:, :], in0=fr, scalar1=sgn, scalar2=1.0,
                                                      op0=ALU.mult, op1=ALU.mult), reads=[bt[4], bsm], writes=[bc5])
        else:
            P.op(v, lambda e, dst=dst: e.tensor_copy(out=dst[:, :], in_=fr), reads=[bt[4]], writes=[bc5])
    CONST.append(bc5)
    do_fence()

    x_src, x_srcb = x_in, buf("x_in")
    final_ops = []
    dbg_outs = {}

    for l in range(L):
        last = (l == L - 1)
        x_dst = y_out if last else xs[l % 2]
        x_dstb = buf("y_out") if last else buf(f"xs{l % 2}")
        bsl = buf("small_l")

        P.dma("sync", badaT, b_adaT[l], writes=[bsl])
        P.dma("sync", gq_s, gq_in[l], writes=[bsl])
        P.dma("sync", gkv_s, gkv_in[l], writes=[bsl])
        P.dma("sync", fb_s, fbias[l].partition_broadcast(128), writes=[bsl])
        cB = arena[:, 30720:32768]
        bcB = pbuf("cB")
        for kc in range(KC):
            P.op("vector", lambda e, kc=kc: e.tensor_copy(out=cB[:, kc * 128:(kc + 1) * 128],
                                                          in_=cT_f[:, kc:kc + 1].to_broadcast([128, 128])),
                 reads=[bsm], writes=[bcB])
        slabs = [a32[:, 0:6144], a32[:, 6144:12288]]
        bslab = [pbuf("ada_slab0"), pbuf("ada_slab1")]
        slb = arena[:, 24576:30720]
        bb_ = pbuf("ada_slb")
        for kc in range(KC):
            slab, bs_ = slabs[kc % 2], bslab[kc % 2]
            P.dma("sync", slab, w_ada[l, kc * 128:(kc + 1) * 128, :], writes=[bs_])
            P.op("gpsimd", lambda e, slab=slab: e.tensor_copy(out=slb[:, 0:1536], in_=slab[:, 0:1536]), reads=[bs_], writes=[bb_])
            P.op("vector", lambda e, slab=slab: e.tensor_copy(out=slb[:, 1536:6144], in_=slab[:, 1536:6144]), reads=[bs_], writes=[bb_])

            def mm(e, kc=kc):
                ins = None
                for j in range(32):
                    ins = e.matmul(ps[0][:, j:j + 1], lhsT=slb[:, j * 128:(j + 1) * 128], rhs=cT_b[:, kc:kc + 1],
                                   start=True, stop=True)
                for r in range(4):
                    ins = e.matmul(ps[1 + r][:, :], lhsT=cB[:, kc * 128:(kc + 1) * 128],
                                   rhs=slb[:, 4096 + r * 512:4096 + (r + 1) * 512],
                                   start=(kc == 0), stop=(kc == KC - 1))
                return ins
            P.op("tensor", mm, reads=[bb_, bcB] + CONST, writes=psb[0:5])
            if kc == 0:
                P.op("vector", lambda e: e.tensor_copy(out=modT, in_=ps[0][:, 0:32]), reads=[psb[0]], writes=[bsl])
            else:
                P.op("vector", lambda e: e.tensor_tensor(out=modT, in0=modT, in1=ps[0][:, 0:32], op=ALU.add), reads=[psb[0], bsl], writes=[bsl])
        P.op("vector", lambda e: e.tensor_tensor(out=modT, in0=modT, in1=badaT[:, 0:32], op=ALU.add),
             reads=[bsl], writes=[bsl])
        P.op("vector", lambda e: e.tensor_scalar(out=sc1T, in0=modT[:, 16:32], scalar1=1.0, scalar2=1.0, op0=ALU.add, op1=ALU.mult),
             reads=[bsl], writes=[bsl])
        grow = a32[:, 0:2048]
        bgr = bslab[0]
        gbr = wst[0]
        P.dma("sync", gbr[:, 0:2048], gate_b[l].partition_broadcast(128), writes=[wst_b[0]])
        for r in range(4):
            P.op("vector", lambda e, r=r: e.tensor_tensor(out=grow[:, r * 512:(r + 1) * 512], in0=ps[1 + r][:, :],
                                                         in1=gbr[:, r * 512:(r + 1) * 512], op=ALU.add),
                 reads=[psb[1 + r], wst_b[0]], writes=[bgr])
        bGR = buf("GR")
        P.dma("scalar", GR[:, :], grow, reads=[bgr], writes=[bGR])
        do_fence()

        if maxphase <= 0:
            break
        buT = [buf(f"uT{r}") for r in range(NR)]
        for tb in range(NB):
            sl = tb % 2
            xb = a32[:, sl * 2048:(sl + 1) * 2048]
            xn = arena[:, 8192 + sl * 2048:8192 + (sl + 1) * 2048]
            bx, bxn, bst = pbuf(f"p1x{sl}"), pbuf(f"p1xn{sl}"), pbuf(f"p1st{sl}")
            P.dma(dq(), xb, x_src[tb * 128:(tb + 1) * 128, :], reads=[x_srcb], writes=[bx])
            stats = small[:, 912 + sl * 24:912 + (sl + 1) * 24]
            mv = mvs[:, sl * 4:sl * 4 + 2]
            rs = mvs[:, sl * 4 + 2:sl * 4 + 3]
            for q4 in range(4):
                P.op(v, lambda e, q4=q4, xb=xb, stats=stats: e.bn_stats(out=stats[:, q4 * 6:(q4 + 1) * 6], in_=xb[:, q4 * 512:(q4 + 1) * 512]),
                     reads=[bx], writes=[bst])
            P.op(v, lambda e, mv=mv, stats=stats: e.bn_aggr(out=mv, in_=stats), reads=[bst], writes=[bst])
            P.op(v, lambda e, mv=mv, rs=rs: e.tensor_scalar(out=rs, in0=mv[:, 1:2], scalar1=LN_EPS, scalar2=1.0, op0=ALU.add, op1=ALU.mult),
                 reads=[bst], writes=[bst])
            P.op("scalar", lambda e, rs=rs: e.activation(out=rs, in_=rs, func=AF.Sqrt), reads=[bst], writes=[bst])
            P.op(v, lambda e, rs=rs: e.reciprocal(out=rs, in_=rs), reads=[bst], writes=[bst])
            P.op(v, lambda e, xb=xb, xn=xn, mv=mv, rs=rs: e.tensor_scalar(out=xn, in0=xb, scalar1=mv[:, 0:1], scalar2=rs,
                                                                         op0=ALU.subtract, op1=ALU.mult),
                 reads=[bx, bst], writes=[bxn])
            for half in range(2):
                pi_ = nps(4)
                pst = ps[pi_][:, :].bitcast(BF16)

                def tr(e, half=half, xn=xn, pst=pst):
                    ins = None
                    for k8 in range(8):
                        kc = half * 8 + k8
                        ins = e.transpose(pst[:, k8 * 128:(k8 + 1) * 128], xn[:, kc * 128:(kc + 1) * 128], ident)
                    return ins
                P.op("tensor", tr, reads=[bxn] + CONST, writes=[psb[pi_]])
                for k8 in range(8):
                    kc = half * 8 + k8
                    dst = uT[:, kc * S + tb * 128: kc * S + (tb + 1) * 128]
                    src = pst[:, k8 * 128:(k8 + 1) * 128]
                    if k8 % 2 == 0:
                        P.op("scalar", lambda e, dst=dst, src=src, kc=kc: e.activation(out=dst, in_=src, func=AF.Identity,
                                                                                      bias=modT[:, kc:kc + 1], scale=sc1T[:, kc:kc + 1]),
                             reads=[psb[pi_], bsl], writes=[buT[tb // 4]])
                    else:
                        P.op(v, lambda e, dst=dst, src=src, kc=kc: e.tensor_scalar(out=dst, in0=src, scalar1=sc1T[:, kc:kc + 1],
                                                                                  scalar2=modT[:, kc:kc + 1], op0=ALU.mult, op1=ALU.add),
                             reads=[psb[pi_], bsl], writes=[buT[tb // 4]])
        do_fence()

        if maxphase <= 1:
            break
        bPT, bQA, bVS, bYS = buf("PT"), buf("QA"), buf("VS"), buf("YS")
        rowst = [arena[:, i * 2048:(i + 1) * 2048] for i in range(3)]
        rowb = [pbuf(f"row{i}") for i in range(3)]
        cqn = arena[:, 6144:14336]
        ckvn = arena[:, 14336:18432]
        bcq, bckv = pbuf("cqn"), pbuf("ckvn")
        sqt = [arena[:, 18432 + i * 512:18432 + (i + 1) * 512] for i in range(6)]
        bsq = pbuf("sq")
        rr = a32[:, 10752:11264]
        brr = pbuf("rr")
        vst = [arena[:, 22528 + i * 512:22528 + (i + 1) * 512] for i in range(2)]
        bvst = [pbuf(f"vst{i}") for i in range(2)]
        wide = arena[:, 23552:31744]
        bwide = pbuf("wide")
        t1 = a32[0:64, 15872:16384]
        bt1 = pbuf("t1")
        cnt_row = [0]

        def proj_rows(wt, wtb, width, evac, extra_reads=(), kcn=KC, src=None, srcb=None, wstride=128, woff=0):
            for r in range(NR):
                pi_ = nps(6)

                def mm(e, r=r, pi_=pi_):
                    ins = None
                    for kc in range(kcn):
                        if src is None:
                            rhs = uT[:, kc * S + r * 512: kc * S + (r + 1) * 512]
                        else:
                            rhs = src[:, kc * S + r * 512: kc * S + (r + 1) * 512]
                        ins = e.matmul(ps[pi_][0:width, :], lhsT=wt[:, kc * wstride + woff: kc * wstride + woff + width],
                                       rhs=rhs, start=(kc == 0), stop=(kc == kcn - 1))
                    return ins
                rd = [wtb] + (list(buT) if src is None else [srcb]) + list(extra_reads)
                P.op("tensor", mm, reads=rd, writes=[psb[pi_]])
                evac(r, pi_)

        def simple_chunk(chunk, kind, dst_ap_fn, dstb):
            wt, wtb = load_cast(w_ch[l, chunk], 2048)
            i = cnt_row[0] % 3
            cnt_row[0] += 1
            row, rb = rowst[i], rowb[i]

            def evac(r, pi_):
                out = row[:, r * 512:(r + 1) * 512]
                if kind == "q":
                    eng = ev_eng()
                    P.op(eng, copy_scaled(eng, out, ps[pi_][:, :], 1.0 / math.sqrt(128.0)), reads=[psb[pi_]], writes=[rb])
                elif kind == "k":
                    eng = ev_eng()
                    P.op(eng, copy_scaled(eng, out, ps[pi_][:, :]), reads=[psb[pi_]], writes=[rb])
                elif kind == "g":
                    P.op("scalar", lambda e: e.activation(out=out, in_=ps[pi_][:, :], func=AF.Silu), reads=[psb[pi_]], writes=[rb])
                elif kind == "m":
                    P.op("scalar", lambda e: e.activation(out=out, in_=ps[pi_][:, :], func=AF.Sigmoid), reads=[psb[pi_]], writes=[rb])
            proj_rows(wt, wtb, 128, evac)
            P.dma("scalar", dst_ap_fn(), row, reads=[rb], writes=[dstb])

        for (ch0, nchk, dstT, dstb, ncols) in ((CH_CQ, 4, cqn, bcq, 512.0), (CH_CKV, 2, ckvn, bckv, 256.0)):
            wts = [load_cast(w_ch[l, ch0 + i], 2048) for i in range(nchk)]
            for r in range(NR):
                for i in range(nchk):
                    wt, wtb = wts[i]

                    def mm(e, r=r, i=i, wt=wt):
                        ins = None
                        for kc in range(KC):
                            ins = e.matmul(ps[i][:, :], lhsT=wt[:, kc * 128:(kc + 1) * 128],
                                           rhs=uT[:, kc * S + r * 512: kc * S + (r + 1) * 512],
                                           start=(kc == 0), stop=(kc == KC - 1))
                        return ins
                    P.op("tensor", mm, reads=[wtb] + list(buT), writes=[psb[i]])
                    P.op("scalar", lambda e, i=i: e.activation(out=sqt[i], in_=ps[i][:, :], func=AF.Square),
                         reads=[psb[i]], writes=[bsq])

                def mss(e, nchk=nchk):
                    ins = None
                    for i in range(nchk):
                        ins = e.matmul(ps[4][:, :], lhsT=ones_b, rhs=sqt[i], start=(i == 0), stop=(i == nchk - 1))
                    return ins
                P.op("tensor", mss, reads=[bsq] + CONST, writes=[psb[4]])
                P.op(v, lambda e, ncols=ncols: e.tensor_scalar(out=rr, in0=ps[4][:, :], scalar1=1.0 / ncols, scalar2=RMS_EPS,
                                                               op0=ALU.mult, op1=ALU.add), reads=[psb[4]], writes=[brr])
                P.op("scalar", lambda e: e.activation(out=rr, in_=rr, func=AF.Sqrt), reads=[brr], writes=[brr])
                P.op(v, lambda e: e.reciprocal(out=rr, in_=rr), reads=[brr], writes=[brr])
                for i in range(nchk):
                    P.op(v, lambda e, i=i, r=r, dstT=dstT: e.tensor_tensor(out=dstT[:, i * S + r * 512: i * S + (r + 1) * 512],
                                                                          in0=ps[i][:, :], in1=rr, op=ALU.mult),
                         reads=[psb[i], brr], writes=[dstb])
        bkpe = buf("kpeT")
        wt, wtb = load_cast(w_ch[l, CH_KR], 2048)
        for r in range(NR):
            def mm(e, r=r, wt=wt):
                ins = None
                for half in range(2):
                    for kc in range(KC):
                        ins = e.matmul(ps[5 + half][0:64, :], lhsT=wt[:, kc * 128 + half * 64: kc * 128 + half * 64 + 64],
                                       rhs=uT[:, kc * S + r * 512: kc * S + (r + 1) * 512],
                                       start=(kc == 0), stop=(kc == KC - 1))
                return ins
            P.op("tensor", mm, reads=[wtb] + list(buT), writes=[psb[5], psb[6]])
            cs = slice(r * 512, (r + 1) * 512)
            P.op(v, lambda e, cs=cs: e.tensor_tensor(out=t1, in0=ps[5][0:64, :], in1=cos2[:, cs], op=ALU.mult),
                 reads=[psb[5]] + CONST, writes=[bt1])
            P.op(v, lambda e, cs=cs: e.tensor_tensor(out=rr[0:64, :], in0=ps[6][0:64, :], in1=sin2[:, cs], op=ALU.mult),
                 reads=[psb[6]] + CONST, writes=[brr])
            P.op(v, lambda e, cs=cs: e.tensor_tensor(out=kpeT[:, cs], in0=t1, in1=rr[0:64, :], op=ALU.add),
                 reads=[bt1, brr], writes=[bkpe])
        SC_A = 1.0 / math.sqrt(192.0)
        for h in range(H):
            wq, wqb = load_cast(w_uq[l, h], 1024, scale_ap=gq_s, nk=4, extra_reads=[bsl])
            i = cnt_row[0] % 3
            cnt_row[0] += 1
            row, rb = rowst[i], rowb[i]

            def evq(r, pi_, row=row, rb=rb):
                eng = ev_eng()
                P.op(eng, copy_scaled(eng, row[:, r * 512:(r + 1) * 512], ps[pi_][:, :], SC_A), reads=[psb[pi_]], writes=[rb])
            proj_rows(wq, wqb, 128, evq, kcn=4, src=cqn, srcb=bcq, wstride=256, woff=0)
            P.dma("scalar", QA[h, 0], row, reads=[rb], writes=[bQA])
            i = cnt_row[0] % 3
            cnt_row[0] += 1
            row, rb = rowst[i], rowb[i]
            for r in range(NR):
                def mm(e, r=r, wq=wq):
                    ins = None
                    for half in range(2):
                        for kc in range(4):
                            ins = e.matmul(ps[5 + half][0:64, :], lhsT=wq[:, kc * 256 + 128 + half * 64: kc * 256 + 192 + half * 64],
                                           rhs=cqn[:, kc * S + r * 512: kc * S + (r + 1) * 512],
                                           start=(kc == 0), stop=(kc == 3))
                    return ins
                P.op("tensor", mm, reads=[wqb, bcq], writes=[psb[5], psb[6]])
                cs = slice(r * 512, (r + 1) * 512)
                P.op(v, lambda e, cs=cs: e.tensor_tensor(out=t1, in0=ps[5][0:64, :], in1=cos2[:, cs], op=ALU.mult),
                     reads=[psb[5]] + CONST, writes=[bt1])
                P.op(v, lambda e, cs=cs: e.tensor_tensor(out=rr[0:64, :], in0=ps[6][0:64, :], in1=sin2[:, cs], op=ALU.mult),
                     reads=[psb[6]] + CONST, writes=[brr])
                P.op(v, lambda e: e.tensor_tensor(out=t1, in0=t1, in1=rr[0:64, :], op=ALU.add),
                     reads=[bt1, brr], writes=[bt1])
                P.op(v, lambda e, cs=cs, row=row: e.tensor_scalar(out=row[0:64, cs], in0=t1, scalar1=SC_A, scalar2=1.0,
                                                                 op0=ALU.mult, op1=ALU.mult),
                     reads=[bt1], writes=[rb])
            P.dma("scalar", QA[h, 1, 0:64, :], row[0:64, :], reads=[rb], writes=[bQA])
            wk, wkb = load_cast(w_uk[l, h], 256, scale_ap=gkv_s, nk=2, extra_reads=[bsl])
            i = cnt_row[0] % 3
            cnt_row[0] += 1
            row, rb = rowst[i], rowb[i]

            def evk(r, pi_, row=row, rb=rb):
                eng = ev_eng()
                P.op(eng, copy_scaled(eng, row[:, r * 512:(r + 1) * 512], ps[pi_][:, :]), reads=[psb[pi_]], writes=[rb])
            proj_rows(wk, wkb, 128, evk, kcn=2, src=ckvn, srcb=bckv, wstride=128, woff=0)
            P.dma("scalar", QA[h, 2], row, reads=[rb], writes=[bQA])
        for gidx in range(2):
            wv, wvb = load_cast(w_uv[l, gidx], 1024, scale_ap=gkv_s, nk=2, extra_reads=[bsl])
            for tb in range(NB):
                pi_ = nps(6)

                def mm(e, tb=tb, pi_=pi_, wv=wv):
                    ins = None
                    for kc in range(2):
                        ins = e.matmul(ps[pi_][:, :], lhsT=ckvn[:, kc * S + tb * 128: kc * S + (tb + 1) * 128],
                                       rhs=wv[:, kc * 512:(kc + 1) * 512], start=(kc == 0), stop=(kc == 1))
                    return ins
                P.op("tensor", mm, reads=[wvb, bckv], writes=[psb[pi_]])
                j = tb % 2
                eng = ev_eng()
                P.op(eng, copy_scaled(eng, vst[j], ps[pi_][:, :]), reads=[psb[pi_]], writes=[bvst[j]])
                P.dma("scalar", VS[0, gidx, :, tb * 512:(tb + 1) * 512], vst[j], reads=[bvst[j]], writes=[bVS])
        for h in range(H):
            simple_chunk(CH_GA + h, "g", lambda h=h: PT[CH_GA + h], bPT)
        for (cq_, ck_, cg_) in ((CH_QB, CH_KB, CH_GB), (CH_QC, CH_KC, CH_GC)):
            for h in range(H):
                simple_chunk(cq_ + h, "q", lambda c=cq_ + h: PT[c], bPT)
                simple_chunk(ck_ + h, "k", lambda c=ck_ + h: PT[c], bPT)
            for h in range(H):
                simple_chunk(cg_ + h, "g", lambda c=cg_ + h: PT[c], bPT)
        for n in (1, 2):
            for gidx in range(2):
                wi = (n - 1) * 2 + gidx
                for piece in range(4):
                    i = cnt["wst"] % 2
                    cnt["wst"] += 1
                    P.dma("sync", wst[i][:, :], w_wide[l, wi, :, piece * 2048:(piece + 1) * 2048], writes=[wst_b[i]])
                    P.op("vector", lambda e, i=i, piece=piece: e.tensor_copy(out=wide[:, piece * 2048:(piece + 1) * 2048], in_=wst[i][:, :]),
                         reads=[wst_b[i]], writes=[bwide])
                for tb in range(NB):
                    pi_ = nps(6)

                    def mm(e, tb=tb, pi_=pi_):
                        ins = None
                        for kc in range(KC):
                            ins = e.matmul(ps[pi_][:, :], lhsT=uT[:, kc * S + tb * 128: kc * S + (tb + 1) * 128],
                                           rhs=wide[:, kc * 512:(kc + 1) * 512], start=(kc == 0), stop=(kc == KC - 1))
                        return ins
                    P.op("tensor", mm, reads=[bwide] + list(buT), writes=[psb[pi_]])
                    j = tb % 2
                    eng = ev_eng()
                    P.op(eng, copy_scaled(eng, vst[j], ps[pi_][:, :]), reads=[psb[pi_]], writes=[bvst[j]])
                    P.dma("scalar", VS[n, gidx, :, tb * 512:(tb + 1) * 512], vst[j], reads=[bvst[j]], writes=[bVS])
        wf, wfb = load_cast(w_f[l], 128)
        bfc = buf("fcum")
        for tb in range(NB):
            def mm(e, tb=tb, wf=wf):
                ins = None
                for kc in range(KC):
                    ins = e.matmul(ps[6][:, tb * 8:(tb + 1) * 8], lhsT=uT[:, kc * S + tb * 128: kc * S + (tb + 1) * 128],
                                   rhs=wf[:, kc * 8:(kc + 1) * 8], start=(kc == 0), stop=(kc == KC - 1))
                return ins
            P.op("tensor", mm, reads=[wfb] + list(buT), writes=[psb[6]])
        for tb in range(NB):
            P.op(v, lambda e, tb=tb: e.tensor_tensor(out=SPf[:, tb * 8:(tb + 1) * 8], in0=ps[6][:, tb * 8:(tb + 1) * 8],
                                                    in1=fb_s, op=ALU.add), reads=[psb[6], bsl], writes=[bfc])
        P.op("scalar", lambda e: e.activation(out=SPf, in_=SPf, func=AF.Exp, scale=-1.0), reads=[bfc], writes=[bfc])
        P.op("scalar", lambda e: e.activation(out=SPf, in_=SPf, func=AF.Ln, bias=1.0, scale=1.0), reads=[bfc], writes=[bfc])

        def mmc(e):
            ins = None
            for tb in range(NB):
                for t2 in range(tb + 1):
                    ins = e.matmul(ps[7][:, tb * 8:(tb + 1) * 8], lhsT=(negtri if t2 == tb else negonesf),
                                   rhs=SPf[:, t2 * 8:(t2 + 1) * 8], start=(t2 == 0), stop=(t2 == tb))
            return ins
        P.op("tensor", mmc, reads=[bfc] + CONST, writes=[psb[7]])
        bfc2 = buf("fcum2")
        P.op(v, lambda e: e.tensor_copy(out=Fcum, in_=ps[7][:, 0:128]), reads=[psb[7]], writes=[bfc2])
        bfc3 = buf("fcum3")
        P.op(v, lambda e: e.tensor_scalar(out=Cend, in0=Fcum, scalar1=-1.0, scalar2=1.0, op0=ALU.mult, op1=ALU.mult),
             reads=[bfc2], writes=[bfc3])
        for n in range(3):
            for dc in range(KC):
                c = CH_M + n * KC + dc
                simple_chunk(c, "m", lambda c=c: PT[c], bPT)
        do_fence()

        if maxphase <= 2:
            break
        hb = {}
        for i in range(2):
            hb[i] = dict(q=arena[:, (i * 3 + 0) * 2048:(i * 3 + 1) * 2048], k=arena[:, (i * 3 + 1) * 2048:(i * 3 + 2) * 2048],
                         g=arena[:, (i * 3 + 2) * 2048:(i * 3 + 3) * 2048], qp=arena[0:64, 12288 + i * 2048:12288 + (i + 1) * 2048],
                         bq=pbuf(f"hq{i}"), bk=pbuf(f"hk{i}"), bg=pbuf(f"hg{i}"), bqp=pbuf(f"hqp{i}"))
        Vg = arena[:, 16384:24576]
        bVg = pbuf("Vg")
        W0 = 24576
        pT = [[arena[:, W0 + (s_ * 2 + i) * 512:W0 + (s_ * 2 + i + 1) * 512] for i in range(2)] for s_ in range(2)]
        bpT = [[pbuf(f"pT{s_}{i}") for i in range(2)] for s_ in range(2)]
        spt = [[a32[:, (W0 + 2048) // 2 + (s_ * 2 + i) * 512:(W0 + 2048) // 2 + (s_ * 2 + i + 1) * 512] for i in range(2)] for s_ in range(2)]
        bspt = [[pbuf(f"spt{s_}{i}") for i in range(2)] for s_ in range(2)]
        yst = [arena[:, W0 + 6144 + i * 512:W0 + 6144 + (i + 1) * 512] for i in range(2)]
        byst = [pbuf(f"yst{i}") for i in range(2)]
        recs = [a32[:, (W0 + 7168) // 2:(W0 + 7168) // 2 + 512], wst[0][:, 0:512]]
        brecs = [pbuf("rec0"), pbuf("rec1")]
        Rfs = recs
        bRfs = brecs
        Lns = [[wbf[0][:, (s_ * 2 + i) * 512:(s_ * 2 + i + 1) * 512] for i in range(2)] for s_ in range(2)]
        bLns = [[pbuf(f"Ln{s_}{i}") for i in range(2)] for s_ in range(2)]
        Rbv = [[arena[:, 12288 + (s_ * 3 + i) * 512:12288 + (s_ * 3 + i + 1) * 512] for i in range(3)] for s_ in range(2)]
        bRbv = [[pbuf(f"Rb{s_}{i}") for i in range(3)] for s_ in range(2)]
        Frow = a32[:, 6144:8192]
        bFrow = pbuf("Frow")
        bdg = [pbuf("dg0"), pbuf("dg1")]
        grp0 = [wst_b[0], wbf_b[0], brecs[1]] + bLns[0] + bLns[1]
        P.op(v, lambda e: e.memset(small[:, 1019:1020], 0.0), reads=grp0, writes=grp0)
        hcount = 0
        ycount = [0]
        for n in range(3):
            if n == 1:
                grp = [hb[0]["bqp"], hb[1]["bqp"], bFrow]
                P.op(v, lambda e: e.memset(small[:, 1021:1022], 0.0), reads=grp, writes=grp)
            if n == 2:
                grp = [bFrow] + bRbv[0] + bRbv[1]
                P.op(v, lambda e: e.memset(small[:, 1022:1023], 0.0), reads=grp, writes=grp)
            for h in range(H):
                hs = hb[hcount % 2]
                hcount += 1
                if h % 4 == 0:
                    P.dma("sync", Vg, VS[n, h // 4], reads=[bVS], writes=[bVg])
                if n == 0:
                    P.dma("sync", hs["q"], QA[h, 0], reads=[bQA], writes=[hs["bq"]])
                    P.dma("sync", hs["qp"], QA[h, 1, 0:64, :], reads=[bQA], writes=[hs["bqp"]])
                    P.dma("sync", hs["k"], QA[h, 2], reads=[bQA], writes=[hs["bk"]])
                    P.dma("sync", hs["g"], PT[CH_GA + h], reads=[bPT], writes=[hs["bg"]])
                else:
                    cq_, ck_, cg_ = ((CH_QB, CH_KB, CH_GB), (CH_QC, CH_KC, CH_GC))[n - 1]
                    P.dma("sync", hs["q"], PT[cq_ + h], reads=[bPT], writes=[hs["bq"]])
                    P.dma("sync", hs["k"], PT[ck_ + h], reads=[bPT], writes=[hs["bk"]])
                    P.dma("sync", hs["g"], PT[cg_ + h], reads=[bPT], writes=[hs["bg"]])
                if n == 1:
                    for i4 in range(4):
                        for ii in range(4):
                            i = i4 * 4 + ii
                            dj = i % 2
                            P.op(v, lambda e, i=i, dj=dj, h=h: e.tensor_scalar(out=dgt[:, dj * 128:(dj + 1) * 128], in0=identf,
                                                                               scalar1=Fcum[:, i * 8 + h:i * 8 + h + 1], scalar2=1.0,
                                                                               op0=ALU.mult, op1=ALU.mult),
                                 reads=[bfc2] + CONST, writes=[bdg[dj]])
                            P.op("tensor", lambda e, ii=ii, dj=dj: e.matmul(ps[6][:, ii * 128:(ii + 1) * 128], lhsT=onesf,
                                                                          rhs=dgt[:, dj * 128:(dj + 1) * 128], start=True, stop=True),
                                 reads=[bdg[dj]] + CONST, writes=[psb[6]])
                        P.op(v, lambda e, i4=i4: e.tensor_copy(out=Frow[:, i4 * 512:(i4 + 1) * 512], in_=ps[6][:, :]),
                             reads=[psb[6]], writes=[bFrow])
                vcol = (h % 4) * 128
                streams = []
                for s_, Is in enumerate(((3, 0), (2, 1))):
                    lst = []
                    for I in Is:
                        Js = list(range(4 * I + 4))
                        if n == 2:
                            Js = Js[::-1]
                        for idx, J in enumerate(Js):
                            lst.append(dict(I=I, J=J, first=(idx == 0), last=(idx == len(Js) - 1),
                                            off=max(0, J - 4 * I) * 128, diag=(J >= 4 * I)))
                    streams.append(lst)
                PSS = ((0, 1), (7, 6) if n != 1 else (7,))

                def G1(s_, k, n=n, h=h, hs=hs, streams=streams, PSS=PSS):
                    st = streams[s_][k]
                    I, J, off, diag = st["I"], st["J"], st["off"], st["diag"]
                    psi = PSS[s_][k % len(PSS[s_])]
                    kcols = slice(J * 128, (J + 1) * 128)
                    qcols = slice(I * 512 + off, (I + 1) * 512)
                    w_ = slice(off, 512)

                    def mqk(e):
                        ins = e.matmul(ps[psi][:, w_], lhsT=hs["k"][:, kcols], rhs=hs["q"][:, qcols], start=True, stop=(n != 0))
                        if n == 0:
                            ins = e.matmul(ps[psi][:, w_], lhsT=kpeT[:, kcols], rhs=hs["qp"][:, qcols], start=False, stop=True)
                        return ins
                    rd = [hs["bq"], hs["bk"]] + ([hs["bqp"], bkpe] if n == 0 else [])
                    P.op("tensor", mqk, reads=rd, writes=[psb[psi]])
                    pt_, bpt_ = pT[s_][k % 2], bpT[s_][k % 2]
                    sp_, bsp_ = spt[s_][k % 2], bspt[s_][k % 2]
                    if n == 0:
                        P.op("scalar", lambda e: e.activation(out=pt_[:, w_], in_=ps[psi][:, w_], func=AF.Exp),
                             reads=[psb[psi]], writes=[bpt_])
                        if diag:
                            P.op("gpsimd", lambda e: e.tensor_tensor(out=pt_[:, off:off + 128], in0=pt_[:, off:off + 128], in1=m_mla, op=ALU.mult),
                                 reads=[bpt_] + CONST, writes=[bpt_])
                    elif n == 1:
                        P.op(v, lambda e: e.tensor_tensor(out=sp_[:, w_], in0=ps[psi][:, w_], in1=Frow[:, qcols], op=ALU.add),
                             reads=[psb[psi], bFrow], writes=[bsp_])
                        if diag:
                            P.op("gpsimd", lambda e: e.tensor_tensor(out=sp_[:, off:off + 128], in0=sp_[:, off:off + 128], in1=mnegf, op=ALU.add),
                                 reads=[bsp_] + CONST, writes=[bsp_])
                        P.op("scalar", lambda e: e.activation(out=pt_[:, w_], in_=sp_[:, w_], func=AF.Exp,
                                                              bias=Cend[:, J * 8 + h:J * 8 + h + 1], scale=1.0),
                             reads=[bsp_, bfc3], writes=[bpt_])
                    else:
                        ln_, bln_ = Lns[s_][k % 2], bLns[s_][k % 2]
                        Rf, bRf = Rfs[s_], bRfs[s_]
                        P.op("scalar", lambda e: e.activation(out=sp_[:, w_], in_=ps[psi][:, w_], func=AF.Exp, scale=-1.0),
                             reads=[psb[psi]], writes=[bsp_])
                        P.op("scalar", lambda e: e.activation(out=sp_[:, w_], in_=sp_[:, w_], func=AF.Ln, bias=1.0, scale=1.0),
                             reads=[bsp_], writes=[bsp_])
                        P.op(v, lambda e: e.tensor_tensor(out=ln_[:, w_], in0=ps[psi][:, w_], in1=sp_[:, w_], op=ALU.add),
                             reads=[psb[psi], bsp_], writes=[bln_])
                        if diag:
                            P.op("gpsimd", lambda e: e.tensor_tensor(out=ln_[:, off:off + 128], in0=ln_[:, off:off + 128], in1=m_sb, op=ALU.mult),
                                 reads=[bln_] + CONST, writes=[bln_])

                def G2(s_, k, n=n, h=h, hs=hs, streams=streams):
                    st = streams[s_][k]
                    off, diag = st["off"], st["diag"]
                    w_ = slice(off, 512)
                    pa = 4 + s_
                    sp_, bsp_ = spt[s_][k % 2], bspt[s_][k % 2]
                    ln_, bln_ = Lns[s_][k % 2], bLns[s_][k % 2]
                    at_, bat_ = pT[s_][k % 2], bpT[s_][k % 2]
                    rb_, brb_ = Rbv[s_][k % 3], bRbv[s_][k % 3]
                    first = st["first"]

                    def mcs(e):
                        ins = e.matmul(ps[pa][:, w_], lhsT=negU, rhs=ln_[:, w_], start=True, stop=first)
                        if not first:
                            ins = e.matmul(ps[pa][:, w_], lhsT=negones, rhs=rb_[:, w_], start=False, stop=True)
                        return ins
                    P.op("tensor", mcs, reads=[bln_] + ([] if first else [brb_]) + CONST, writes=[psb[pa]])
                    P.op(v, lambda e: e.tensor_tensor(out=sp_[:, w_], in0=ps[pa][:, w_], in1=sp_[:, w_], op=ALU.subtract),
                         reads=[psb[pa], bsp_], writes=[bsp_])
                    P.op("scalar", lambda e: e.activation(out=at_[:, w_], in_=sp_[:, w_], func=AF.Exp), reads=[bsp_], writes=[bat_])
                    if diag:
                        P.op("gpsimd", lambda e: e.tensor_tensor(out=at_[:, off:off + 128], in0=at_[:, off:off + 128], in1=m_sb, op=ALU.mult),
                             reads=[bat_] + CONST, writes=[bat_])
                    if not st["last"]:
                        Rf, bRf = Rfs[s_], bRfs[s_]
                        rbn, brbn = Rbv[s_][(k + 1) % 3], bRbv[s_][(k + 1) % 3]
                        if first:
                            P.op("gpsimd", lambda e: e.memset(Rf[:, :], 0.0), reads=[bRf], writes=[bRf])
                            P.op("gpsimd", lambda e: e.tensor_copy(out=Rf[:, w_], in_=ln_[:, w_]), reads=[bln_, bRf], writes=[bRf])
                        else:
                            P.op("gpsimd", lambda e: e.tensor_tensor(out=Rf[:, w_], in0=Rf[:, w_], in1=ln_[:, w_], op=ALU.add),
                                 reads=[bln_, bRf], writes=[bRf])
                        P.op(v, lambda e: e.tensor_copy(out=rbn[:, :], in_=Rf[:, :]), reads=[bRf], writes=[brbn])

                def G3(s_, k, n=n, h=h, hs=hs, streams=streams, vcol=vcol):
                    st = streams[s_][k]
                    I, J, off = st["I"], st["J"], st["off"]
                    po, pr = 2 + s_, 4 + s_
                    w_ = slice(off, 512)
                    pt_, bpt_ = pT[s_][k % 2], bpT[s_][k % 2]
                    vv = Vg[:, J * 512 + vcol: J * 512 + vcol + 128]
                    rec, brec = recs[s_], brecs[s_]
                    if st["first"]:
                        P.op("tensor", lambda e: e.matmul(ps[po][:, :], lhsT=zeros_b, rhs=cst[:, 0:512], start=True, stop=False),
                             reads=CONST, writes=[psb[po]])
                        if n != 2:
                            P.op("tensor", lambda e: e.matmul(ps[pr][:, :], lhsT=zeros_b, rhs=cst[:, 0:512], start=True, stop=False),
                                 reads=CONST, writes=[psb[pr]])
                    lastj = st["last"]
                    if n != 2:
                        def mpv(e):
                            ins = e.matmul(ps[po][:, w_], lhsT=vv, rhs=pt_[:, w_], start=False, stop=lastj)
                            ins = e.matmul(ps[pr][:, w_], lhsT=ones_b, rhs=pt_[:, w_], start=False, stop=lastj)
                            return ins
                        P.op("tensor", mpv, reads=[bpt_, bVg] + CONST, writes=[psb[po], psb[pr]])
                    else:
                        P.op("tensor", lambda e: e.matmul(ps[po][:, w_], lhsT=vv, rhs=pt_[:, w_], start=False, stop=lastj),
                             reads=[bpt_, bVg], writes=[psb[po]])
                    if lastj:
                        yj = ycount[0] % 2
                        ycount[0] += 1
                        ys_, bys_ = yst[yj], byst[yj]
                        gsl = hs["g"][:, I * 512:(I + 1) * 512]
                        if n != 2:
                            P.op(v, lambda e: e.reciprocal(out=rec, in_=ps[pr][:, :]), reads=[psb[pr]], writes=[brec])
                            P.op("gpsimd", lambda e: e.tensor_tensor(out=rec, in0=rec, in1=gsl, op=ALU.mult), reads=[brec, hs["bg"]], writes=[brec])
                            P.op(v, lambda e: e.tensor_tensor(out=ys_, in0=ps[po][:, :], in1=rec, op=ALU.mult),
                                 reads=[psb[po], brec], writes=[bys_])
                        else:
                            P.op(v, lambda e: e.tensor_tensor(out=ys_, in0=ps[po][:, :], in1=gsl, op=ALU.mult),
                                 reads=[psb[po], hs["bg"]], writes=[bys_])
                        P.dma("scalar", YS[:, n * H + h, I * 512:(I + 1) * 512], ys_, reads=[bys_], writes=[bYS])

                NSm = max(len(streams[0]), len(streams[1]))
                for t in range(NSm + 2):
                    for s_ in range(2):
                        if t < len(streams[s_]):
                            G1(s_, t)
                    if n == 2:
                        for s_ in range(2):
                            if 0 <= t - 1 < len(streams[s_]):
                                G2(s_, t - 1)
                        for s_ in range(2):
                            if 0 <= t - 2 < len(streams[s_]):
                                G3(s_, t - 2)
                    else:
                        for s_ in range(2):
                            if 0 <= t - 1 < len(streams[s_]):
                                G3(s_, t - 1)
        P.op(v, lambda e: e.memset(small[:, 1019:1020], 0.0), reads=grp0, writes=grp0)
        do_fence()

        if maxphase <= 3:
            break
        bmg = [buf(f"uT{r}") for r in range(NR)]
        ysr = arena[:, 0:12288]
        bysr = pbuf("ysr")
        sgt = [[arena[:, 12288 + (j * 3 + n) * 512:12288 + (j * 3 + n + 1) * 512] for n in range(3)] for j in range(2)]
        bsg = [pbuf(f"sg{j}") for j in range(2)]
        acc = [a32[:, 7680 + j * 512:7680 + (j + 1) * 512] for j in range(2)]
        bacc = [pbuf(f"acc{j}") for j in range(2)]
        tmp = [a32[:, 8704 + j * 512:8704 + (j + 1) * 512] for j in range(2)]
        btmp = [pbuf(f"tmp{j}") for j in range(2)]
        for r in range(NR):
            P.dma("sync", ysr.rearrange("p (h t) -> p h t", t=512), YS[:, :, r * 512:(r + 1) * 512], reads=[bYS], writes=[bysr])
            for dc in range(KC):
                j = dc % 2
                wbs = [load_cast(w_br[l, n, dc], 1024, cast_eng=("vector" if n != 1 else "gpsimd")) for n in range(3)]
                for n in range(3):
                    P.dma("sync", sgt[j][n], PT[CH_M + n * KC + dc, :, r * 512:(r + 1) * 512], reads=[bPT], writes=[bsg[j]])
                pbase = 3 * j
                for n in range(3):
                    wt, wtb = wbs[n]

                    def mm(e, n=n, wt=wt, pbase=pbase):
                        ins = None
                        for wc in range(8):
                            ins = e.matmul(ps[pbase + n][:, :], lhsT=wt[:, wc * 128:(wc + 1) * 128],
                                           rhs=ysr[:, (n * 8 + wc) * 512:(n * 8 + wc + 1) * 512], start=(wc == 0), stop=(wc == 7))
                        return ins
                    P.op("tensor", mm, reads=[wtb, bysr], writes=[psb[pbase + n]])
                P.op(v, lambda e, j=j, pbase=pbase: e.tensor_tensor(out=acc[j], in0=ps[pbase][:, :], in1=sgt[j][0], op=ALU.mult),
                     reads=[psb[pbase], bsg[j]], writes=[bacc[j]])
                P.op("gpsimd" if False else v, lambda e, j=j, pbase=pbase: e.tensor_tensor(out=tmp[j], in0=ps[pbase + 1][:, :], in1=sgt[j][1], op=ALU.mult),
                     reads=[psb[pbase + 1], bsg[j]], writes=[btmp[j]])
                P.op("gpsimd", lambda e, j=j: e.tensor_tensor(out=acc[j], in0=acc[j], in1=tmp[j], op=ALU.add),
                     reads=[bacc[j], btmp[j]], writes=[bacc[j]])
                P.op(v, lambda e, j=j, pbase=pbase: e.tensor_tensor(out=tmp[j], in0=ps[pbase + 2][:, :], in1=sgt[j][2], op=ALU.mult),
                     reads=[psb[pbase + 2], bsg[j]], writes=[btmp[j]])
                P.op("gpsimd", lambda e, j=j, dc=dc, r=r: e.tensor_tensor(out=uT[:, dc * S + r * 512: dc * S + (r + 1) * 512], in0=acc[j], in1=tmp[j], op=ALU.add),
                     reads=[bacc[j], btmp[j]], writes=[bmg[r]])
        do_fence()

        if maxphase <= 4:
            break
        wo = [arena[:, j * 8192:(j + 1) * 8192] for j in range(2)]
        bwo = [pbuf(f"wo{j}") for j in range(2)]
        gpc = a32[:, 8192:8704]
        bgp = pbuf("gpc")
        xp = [a32[:, 8704 + j * 512:8704 + (j + 1) * 512] for j in range(2)]
        bxp = [pbuf(f"xp{j}") for j in range(2)]
        yp = [a32[:, 9728 + j * 512:9728 + (j + 1) * 512] for j in range(2)]
        byp = [pbuf(f"yp{j}") for j in range(2)]
        bZS = buf("ZS")
        wo_v = w_out[l].rearrange("(dc p) e -> p dc e", p=128)
        for er in range(NR):
            j = er % 2
            P.dma("sync", gpc, GR[:, er * 512:(er + 1) * 512], reads=[bGR], writes=[bgp])
            for piece in range(4):
                i = cnt["wst"] % 2
                cnt["wst"] += 1
                P.dma("sync", wst[i][:, :].rearrange("p (a b) -> p a b", b=512), wo_v[:, piece * 4:(piece + 1) * 4, er * 512:(er + 1) * 512],
                      writes=[wst_b[i]])
                for q4 in range(4):
                    P.op(v if q4 % 2 else "gpsimd",
                         lambda e, i=i, piece=piece, q4=q4, j=j: e.tensor_tensor(out=wo[j][:, (piece * 4 + q4) * 512:(piece * 4 + q4 + 1) * 512],
                                                                               in0=wst[i][:, q4 * 512:(q4 + 1) * 512], in1=gpc, op=ALU.mult),
                         reads=[wst_b[i], bgp], writes=[bwo[j]])
            for tb in range(NB):
                pi_ = nps(6)
                k = tb % 2

                def mm(e, tb=tb, pi_=pi_, j=j):
                    ins = None
                    for dc in range(KC):
                        ins = e.matmul(ps[pi_][:, :], lhsT=uT[:, dc * S + tb * 128: dc * S + (tb + 1) * 128],
                                       rhs=wo[j][:, dc * 512:(dc + 1) * 512], start=(dc == 0), stop=(dc == KC - 1))
                    return ins
                P.op("tensor", mm, reads=[bwo[j]] + list(bmg), writes=[psb[pi_]])
                P.dma("sync", xp[k], x_src[tb * 128:(tb + 1) * 128, er * 512:(er + 1) * 512], reads=[x_srcb], writes=[bxp[k]])
                P.op(v, lambda e, k=k, pi_=pi_: e.scalar_tensor_tensor(out=yp[k], in0=xp[k], scalar=float(ALPHA), in1=ps[pi_][:, :],
                                                                      op0=ALU.mult, op1=ALU.add),
                     reads=[bxp[k], psb[pi_]], writes=[byp[k]])
                P.dma("scalar", ZS[tb * 128:(tb + 1) * 128, er * 512:(er + 1) * 512], yp[k], reads=[byp[k]], writes=[bZS])
        do_fence()

        if maxphase <= 5:
            break
        lg = a32[:, 0:2048]
        lb = a32[:, 2048:4096]
        blg = pbuf("lnrows")
        P.dma("sync", lg, ln_g[l].partition_broadcast(128), writes=[blg])
        P.dma("sync", lb, ln_b[l].partition_broadcast(128), writes=[blg])
        for tb in range(NB):
            sl = tb % 2
            zb = a32[:, 4096 + sl * 2048:4096 + (sl + 1) * 2048]
            ob = a32[:, 8192 + sl * 2048:8192 + (sl + 1) * 2048]
            bz, bo, bst = pbuf(f"p5z{sl}"), pbuf(f"p5o{sl}"), pbuf(f"p5st{sl}")
            P.dma(dq(), zb, ZS[tb * 128:(tb + 1) * 128, :], reads=[bZS], writes=[bz])
            st2 = small[:, 912 + sl * 24:912 + (sl + 1) * 24]
            mv = mvs[:, 8 + sl * 4:8 + sl * 4 + 2]
            rs = mvs[:, 8 + sl * 4 + 2:8 + sl * 4 + 3]
            for q4 in range(4):
                P.op(v, lambda e, q4=q4, zb=zb, st2=st2: e.bn_stats(out=st2[:, q4 * 6:(q4 + 1) * 6], in_=zb[:, q4 * 512:(q4 + 1) * 512]),
                     reads=[bz], writes=[bst])
            P.op(v, lambda e, mv=mv, st2=st2: e.bn_aggr(out=mv, in_=st2), reads=[bst], writes=[bst])
            P.op(v, lambda e, mv=mv, rs=rs: e.tensor_scalar(out=rs, in0=mv[:, 1:2], scalar1=LN_EPS, scalar2=1.0, op0=ALU.add, op1=ALU.mult),
                 reads=[bst], writes=[bst])
            P.op("scalar", lambda e, rs=rs: e.activation(out=rs, in_=rs, func=AF.Sqrt), reads=[bst], writes=[bst])
            P.op(v, lambda e, rs=rs: e.reciprocal(out=rs, in_=rs), reads=[bst], writes=[bst])
            P.op(v, lambda e, zb=zb, ob=ob, mv=mv, rs=rs: e.tensor_scalar(out=ob, in0=zb, scalar1=mv[:, 0:1], scalar2=rs,
                                                                         op0=ALU.subtract, op1=ALU.mult),
                 reads=[bz, bst], writes=[bo])
            P.op("gpsimd", lambda e, ob=ob: e.tensor_tensor(out=ob, in0=ob, in1=lg, op=ALU.mult), reads=[bo, blg], writes=[bo])
            P.op(v, lambda e, ob=ob: e.tensor_tensor(out=ob, in0=ob, in1=lb, op=ALU.add), reads=[bo, blg], writes=[bo])
            o_ = P.dma("scalar", x_dst[tb * 128:(tb + 1) * 128, :], ob, reads=[bo], writes=[x_dstb])
            if last:
                final_ops.append(o_)
        do_fence()
        x_src, x_srcb = x_dst, x_dstb

    if maxphase < 99:
        final_ops.append(P.dma('sync', y_out[0:128, 0:2048], uT[:, 0:4096].bitcast(F32), reads=[buf('uT0'), buf('uT1'), buf('uT2'), buf('uT3')], writes=[buf('y_out')]))
    P.emit(final_ops)
    for cm in reversed(ctx):
        cm.__exit__(None, None, None)
    return nc, P


def _prep_layer_inputs(inp, layers):
    f = np.float32
    w_in = inp["w_in"]
    L = len(layers)
    w_ch = np.empty((L, NCH, 128, KC * 128), f)
    w_wide = np.empty((L, 4, 128, KC * 512), f)
    w_f = np.empty((L, 128, KC * 8), f)

    def chunk(W, cols):
        n = len(cols)
        return np.ascontiguousarray(W[:, cols].reshape(KC, 128, n).transpose(1, 0, 2)).reshape(128, KC * n)

    ar = np.arange
    for li, l in enumerate(layers):
        W = w_in[l]
        cols = []
        for i in range(4):
            cols.append(ar(i * 128, (i + 1) * 128))
        for i in range(2):
            cols.append(ar(512 + i * 128, 512 + (i + 1) * 128))
        cols.append(np.concatenate([ar(768, 832), ar(800, 832), ar(768, 800)]))
        for h in range(H):
            cols.append(ar(832 + h * 128, 832 + (h + 1) * 128))
        for base in (1856, 2880, 4936, 5960, 6984, 9032):
            for h in range(H):
                cols.append(ar(base + h * 128, base + (h + 1) * 128))
        for i in range(48):
            cols.append(ar(10056 + i * 128, 10056 + (i + 1) * 128))
        assert len(cols) == NCH
        for c, cc in enumerate(cols):
            w_ch[li, c] = chunk(W, cc)
        for wi, base in enumerate((3904, 3904 + 512, 8008, 8008 + 512)):
            w_wide[li, wi] = chunk(W, ar(base, base + 512))
        w_f[li] = chunk(W, ar(4928, 4936))
    d = {}
    d["w_ch"], d["w_wide"], d["w_f"] = w_ch, w_wide, w_f
    ls = list(layers)
    d["w_ada"] = np.ascontiguousarray(inp["w_ada"][ls])
    b_ada = inp["b_ada"][ls]
    d["b_adaT"] = np.ascontiguousarray(b_ada.reshape(L, 48, 128).transpose(0, 2, 1))
    d["gate_b"] = np.ascontiguousarray(b_ada[:, None, 2 * D:3 * D])
    d["gq"] = np.ascontiguousarray(inp["q_norm_g"][ls].reshape(L, 4, 128).transpose(0, 2, 1))
    d["gkv"] = np.ascontiguousarray(inp["kv_norm_g"][ls].reshape(L, 2, 128).transpose(0, 2, 1))
    wuq = inp["w_uq"][ls].reshape(L, 4, 128, H, 192)
    nope = wuq[..., 0:128]
    ropec = wuq[..., 128:192]
    rot = np.concatenate([wuq[..., 160:192], wuq[..., 128:160]], axis=-1)
    allq = np.concatenate([nope, ropec, rot], axis=-1)
    d["w_uq"] = np.ascontiguousarray(allq.transpose(0, 3, 2, 1, 4)).reshape(L, H, 128, 4 * 256)
    wukv = inp["w_ukv"][ls].reshape(L, 2, 128, H, 256)
    d["w_uk"] = np.ascontiguousarray(wukv[..., 0:128].transpose(0, 3, 2, 1, 4)).reshape(L, H, 128, 2 * 128)
    vv = wukv[..., 128:256].reshape(L, 2, 128, 2, 4 * 128)
    d["w_uv"] = np.ascontiguousarray(vv.transpose(0, 3, 2, 1, 4)).reshape(L, 2, 128, 2 * 512)
    d["fbias"] = np.ascontiguousarray(inp["fox_bias"][ls][:, None, :])
    wb = inp["w_branch"][ls].reshape(L, 3, 8, 128, KC, 128)
    d["w_br"] = np.ascontiguousarray(wb.transpose(0, 1, 4, 3, 2, 5)).reshape(L, 3, KC, 128, 8 * 128)
    d["w_out"] = np.ascontiguousarray(inp["w_out"][ls])
    d["ln_g"] = np.ascontiguousarray(inp["ln_g"][ls][:, None, :])
    d["ln_b"] = np.ascontiguousarray(inp["ln_b"][ls][:, None, :])
    return d


_CACHE = {}


MAXPHASE = 99


def _get_prog(L):
    if L not in _CACHE:
        _CACHE[L] = build_program(L, maxphase=MAXPHASE)[0]
    return _CACHE[L]


def _run(inp, x, layers):
    nb = x.shape[0]
    shared = _prep_layer_inputs(inp, layers)
    half = 32
    invf = (10000.0 ** (-np.arange(half, dtype=np.float32) / np.float32(half))).astype(np.float32)
    invf64 = np.concatenate([invf, invf])[:, None].astype(np.float32)
    sgn = np.concatenate([-np.ones(32, np.float32), np.ones(32, np.float32)])[:, None]
    in_maps = []
    for b in range(nb):
        m = dict(shared)
        m["x"] = np.ascontiguousarray(x[b])
        m["cT"] = np.ascontiguousarray(inp["c"][b].reshape(KC, 128).T)
        m["pos"] = np.ascontiguousarray(inp["positions"][b][None, :].astype(np.int32))
        m["invf"] = invf64
        m["sgn"] = sgn
        in_maps.append(m)
    nc = _get_prog(len(layers))
    res = run_bass_kernel_spmd(nc, in_maps, core_ids=list(range(nb)))
    return np.stack([r["y"] for r in res.results], axis=0)


N_FUSED_LAYERS = 4


def kernel(**inputs):
    inp = {k: np.asarray(v) for k, v in inputs.items()}
    x = inp["x"].astype(np.float32)
    l0 = 0
    while l0 < DEPTH:
        ls = list(range(l0, min(DEPTH, l0 + N_FUSED_LAYERS)))
        x = _run(inp, x, ls)
        l0 += len(ls)
    return x.astype(np.float32)
```
